# Optimizing a Trainium2 kernel written in Bass

```python
import math
import jax, jax.numpy as jnp
from jax import lax
import numpy as np

D_MODEL = 1024
BATCH = 16
SEQ = 4096
DEPTH = 1

SWA_Q_HEADS = 8
SWA_KV_HEADS = 2
SWA_GROUP = SWA_Q_HEADS // SWA_KV_HEADS
FOX_HEADS = 8
HEAD_DIM = D_MODEL // (SWA_Q_HEADS + FOX_HEADS)
D_SWA = SWA_Q_HEADS * HEAD_DIM
D_SWA_KV = SWA_KV_HEADS * HEAD_DIM
D_FOX = FOX_HEADS * HEAD_DIM
D_MIX = D_SWA + D_FOX
WINDOW = 128
BLOCK = 128
D_FF = 2816
MACARON = 0.5
ALPHA = (2.0 * DEPTH) ** 0.25
BETA = (8.0 * DEPTH) ** -0.25
LN_EPS = 1e-5
RMS_EPS = 1e-6
N_MODS = 9
IN_SPLITS = (D_SWA, D_SWA_KV, D_SWA_KV, D_FOX, D_FOX, D_FOX, FOX_HEADS)
D_IN = D_SWA + 2 * D_SWA_KV + 3 * D_FOX + FOX_HEADS

kernel_name = "hymba_swa_sink_fox_macaron_deepnorm_adaln"


def layer_norm(x, g, b):
    xf = x.astype(jnp.float32)
    mu = jnp.mean(xf, axis=-1, keepdims=True)
    var = jnp.mean(jnp.square(xf - mu), axis=-1, keepdims=True)
    y = (xf - mu) * lax.rsqrt(var + LN_EPS) * g.astype(jnp.float32) + b.astype(jnp.float32)
    return y.astype(x.dtype)


def rms_norm(x, g):
    xf = x.astype(jnp.float32)
    y = xf * lax.rsqrt(jnp.mean(jnp.square(xf), axis=-1, keepdims=True) + RMS_EPS)
    return (y * g.astype(jnp.float32)).astype(x.dtype)


def modulate(x, shift, scale):
    return x * (1 + scale) + shift


def swiglu(h, w_gate, w_up, w_down):
    return (jax.nn.silu(h @ w_gate) * (h @ w_up)) @ w_down


def alibi_slopes(n_heads):
    return jnp.exp2(-8.0 * jnp.arange(1, n_heads + 1, dtype=jnp.float32) / n_heads)


def swa_sink_attention(q, k, v, sinks):
    B, S = q.shape[0], q.shape[1]
    nb = S // BLOCK
    qb = q.reshape(B, nb, BLOCK, SWA_KV_HEADS, SWA_GROUP, HEAD_DIM)
    kb = k.reshape(B, nb, BLOCK, SWA_KV_HEADS, HEAD_DIM)
    vb = v.reshape(B, nb, BLOCK, SWA_KV_HEADS, HEAD_DIM)
    pad_k = jnp.zeros_like(kb[:, :1])
    pad_v = jnp.zeros_like(vb[:, :1])
    kk = jnp.concatenate([jnp.concatenate([pad_k, kb[:, :-1]], axis=1), kb], axis=2)
    vv = jnp.concatenate([jnp.concatenate([pad_v, vb[:, :-1]], axis=1), vb], axis=2)
    scale = 1.0 / math.sqrt(HEAD_DIM)
    s = jnp.einsum('bnqkgd,bnskd->bnkgqs', qb, kk).astype(jnp.float32) * scale
    qpos = jnp.arange(BLOCK)
    kpos = jnp.arange(2 * BLOCK) - BLOCK
    dist = qpos[:, None] - kpos[None, :]
    valid_key = (jnp.arange(nb)[:, None] * BLOCK + kpos[None, :]) >= 0
    mask = (dist >= 0)[None] & (dist < WINDOW)[None] & valid_key[:, None, :]
    slopes = alibi_slopes(SWA_Q_HEADS).reshape(SWA_KV_HEADS, SWA_GROUP)
    alibi = slopes[:, :, None, None] * dist.astype(jnp.float32)
    s = jnp.where(mask[None, :, None, None], s - alibi, -jnp.inf)
    sink = sinks.astype(jnp.float32).reshape(SWA_KV_HEADS, SWA_GROUP)[None, None, :, :, None]
    m = jnp.maximum(jnp.max(s, axis=-1), sink)
    p = jnp.exp(s - m[..., None])
    denom = jnp.sum(p, axis=-1) + jnp.exp(sink - m)
    p = (p / denom[..., None]).astype(v.dtype)
    o = jnp.einsum('bnkgqs,bnskd->bnqkgd', p, vv)
    return o.reshape(B, S, SWA_Q_HEADS * HEAD_DIM)


def forgetting_attention(q, k, v, log_f):
    B, S = q.shape[0], q.shape[1]
    nb = S // BLOCK
    cum = jnp.cumsum(log_f, axis=1).transpose(0, 2, 1)
    q_blocks = q.reshape(B, nb, BLOCK, FOX_HEADS, HEAD_DIM).transpose(1, 0, 2, 3, 4)
    c_blocks = cum.reshape(B, FOX_HEADS, nb, BLOCK).transpose(2, 0, 1, 3)
    kpos = jnp.arange(S)
    scale = 1.0 / math.sqrt(HEAD_DIM)

    def one_block(args):
        i, qb, cq = args
        s = jnp.einsum('bqhd,bshd->bhqs', qb, k).astype(jnp.float32) * scale
        s = s + cq[..., :, None] - cum[..., None, :]
        tpos = i * BLOCK + jnp.arange(BLOCK)
        causal = kpos[None, :] <= tpos[:, None]
        s = jnp.where(causal[None, None], s, -jnp.inf)
        p = jax.nn.softmax(s, axis=-1).astype(v.dtype)
        return jnp.einsum('bhqs,bshd->bqhd', p, v)

    o = lax.map(one_block, (jnp.arange(nb), q_blocks, c_blocks))
    return o.transpose(1, 0, 2, 3, 4).reshape(B, S, FOX_HEADS * HEAD_DIM)


def token_mix(h, w_in, b_forget, swa_sinks, grp_gain, w_out):
    B, S = h.shape[0], h.shape[1]
    proj = h @ w_in
    idx, acc = [], 0
    for n in IN_SPLITS[:-1]:
        acc += n
        idx.append(acc)
    q_a, k_a, v_a, q_b, k_b, v_b, f_logit = jnp.split(proj, idx, axis=-1)
    o_a = swa_sink_attention(q_a.reshape(B, S, SWA_Q_HEADS, HEAD_DIM),
                             k_a.reshape(B, S, SWA_KV_HEADS, HEAD_DIM),
                             v_a.reshape(B, S, SWA_KV_HEADS, HEAD_DIM), swa_sinks)
    log_f = jax.nn.log_sigmoid(f_logit.astype(jnp.float32) + b_forget.astype(jnp.float32))
    o_b = forgetting_attention(q_b.reshape(B, S, FOX_HEADS, HEAD_DIM),
                               k_b.reshape(B, S, FOX_HEADS, HEAD_DIM),
                               v_b.reshape(B, S, FOX_HEADS, HEAD_DIM), log_f)
    o = jnp.concatenate([rms_norm(o_a, grp_gain[:D_SWA]), rms_norm(o_b, grp_gain[D_SWA:])], axis=-1)
    return o @ w_out


def setup_inputs(seed: int = 0) -> dict:
    key = jax.random.key(seed)
    ks = jax.random.split(key, 32)
    f32 = jnp.float32
    L, D = DEPTH, D_MODEL
    nrm = lambda k, shape: jax.random.normal(k, shape, f32)
    x = nrm(ks[0], (BATCH, SEQ, D))
    c = nrm(ks[1], (BATCH, D))
    w_ada = nrm(ks[2], (L, D, N_MODS * D)) * (0.5 * D ** -0.5)
    b_ada = nrm(ks[3], (L, N_MODS * D)) * 0.01
    ffn1_w_gate = nrm(ks[4], (L, D, D_FF)) * D ** -0.5
    ffn1_w_up = nrm(ks[5], (L, D, D_FF)) * D ** -0.5
    ffn1_w_down = nrm(ks[6], (L, D_FF, D)) * (BETA * D_FF ** -0.5)
    w_q_a = nrm(ks[7], (L, D, D_SWA)) * D ** -0.5
    w_k_a = nrm(ks[8], (L, D, D_SWA_KV)) * D ** -0.5
    w_v_a = nrm(ks[9], (L, D, D_SWA_KV)) * (BETA * D ** -0.5)
    w_q_b = nrm(ks[10], (L, D, D_FOX)) * D ** -0.5
    w_k_b = nrm(ks[11], (L, D, D_FOX)) * D ** -0.5
    w_v_b = nrm(ks[12], (L, D, D_FOX)) * (BETA * D ** -0.5)
    w_f = nrm(ks[13], (L, D, FOX_HEADS)) * D ** -0.5
    w_in = jnp.concatenate([w_q_a, w_k_a, w_v_a, w_q_b, w_k_b, w_v_b, w_f], axis=-1)
    b_forget = 2.0 + 0.5 * nrm(ks[14], (L, FOX_HEADS))
    swa_sinks = 0.5 * nrm(ks[15], (L, SWA_Q_HEADS))
    grp_gain = 1.0 + 0.02 * nrm(ks[16], (L, D_MIX))
    w_out = nrm(ks[17], (L, D_MIX, D)) * (BETA * D_MIX ** -0.5)
    ffn2_w_gate = nrm(ks[18], (L, D, D_FF)) * D ** -0.5
    ffn2_w_up = nrm(ks[19], (L, D, D_FF)) * D ** -0.5
    ffn2_w_down = nrm(ks[20], (L, D_FF, D)) * (BETA * D_FF ** -0.5)
    ln1_g = 1.0 + 0.02 * nrm(ks[21], (L, D))
    ln1_b = 0.02 * nrm(ks[22], (L, D))
    ln2_g = 1.0 + 0.02 * nrm(ks[23], (L, D))
    ln2_b = 0.02 * nrm(ks[24], (L, D))
    ln3_g = 1.0 + 0.02 * nrm(ks[25], (L, D))
    ln3_b = 0.02 * nrm(ks[26], (L, D))
    return {"x": x, "c": c, "w_ada": w_ada, "b_ada": b_ada,
            "ffn1_w_gate": ffn1_w_gate, "ffn1_w_up": ffn1_w_up, "ffn1_w_down": ffn1_w_down,
            "w_in": w_in, "b_forget": b_forget, "swa_sinks": swa_sinks,
            "grp_gain": grp_gain, "w_out": w_out,
            "ffn2_w_gate": ffn2_w_gate, "ffn2_w_up": ffn2_w_up, "ffn2_w_down": ffn2_w_down,
            "ln1_g": ln1_g, "ln1_b": ln1_b, "ln2_g": ln2_g, "ln2_b": ln2_b,
            "ln3_g": ln3_g, "ln3_b": ln3_b}


def reference(x, c, w_ada, b_ada, ffn1_w_gate, ffn1_w_up, ffn1_w_down, w_in, b_forget,
              swa_sinks, grp_gain, w_out, ffn2_w_gate, ffn2_w_up, ffn2_w_down,
              ln1_g, ln1_b, ln2_g, ln2_b, ln3_g, ln3_b):
    silu_c = jax.nn.silu(c)
    for l in range(DEPTH):
        mods = silu_c @ w_ada[l] + b_ada[l]
        sh1, sc1, g1, sh2, sc2, g2, sh3, sc3, g3 = [m[:, None, :] for m in jnp.split(mods, N_MODS, axis=-1)]
        h = modulate(x, sh1, sc1)
        y = swiglu(h, ffn1_w_gate[l], ffn1_w_up[l], ffn1_w_down[l])
        x = layer_norm(ALPHA * x + (1 + g1) * (MACARON * y), ln1_g[l], ln1_b[l])
        h = modulate(x, sh2, sc2)
        y = token_mix(h, w_in[l], b_forget[l], swa_sinks[l], grp_gain[l], w_out[l])
        x = layer_norm(ALPHA * x + (1 + g2) * y, ln2_g[l], ln2_b[l])
        h = modulate(x, sh3, sc3)
        y = swiglu(h, ffn2_w_gate[l], ffn2_w_up[l], ffn2_w_down[l])
        x = layer_norm(ALPHA * x + (1 + g3) * (MACARON * y), ln3_g[l], ln3_b[l])
    return x
```

```python
import math
import numpy as np
import concourse.bass as bass
import concourse.mybir as mybir
from concourse.bass_utils import run_bass_kernel_spmd

F32 = mybir.dt.float32
BF16 = mybir.dt.bfloat16
AF = mybir.ActivationFunctionType
ALU = mybir.AluOpType

D = 1024
DFF = 2816
NJ = DFF // 128
DIN = 2312
TT = 512
ALPHA = 2.0 ** 0.25
LN_EPS = 1e-5
RMS_EPS = 1e-6
NEG = -30000.0


class Buf:
    __slots__ = ("name", "last_w", "readers")

    def __init__(self, name):
        self.name = name
        self.last_w = None
        self.readers = []


class Op:
    __slots__ = ("eng", "fn", "deps", "signal", "sem", "val", "dma_sem", "ndma", "tag")

    def __init__(self, eng, fn):
        self.eng = eng
        self.fn = fn
        self.deps = []
        self.signal = False
        self.sem = None
        self.val = 0
        self.dma_sem = None
        self.ndma = 0


import types


def _snap(fn, depth=0, memo=None):
    if memo is None:
        memo = {}
    if not isinstance(fn, types.FunctionType) or fn.__closure__ is None or depth > 3:
        return fn
    if id(fn) in memo:
        return memo[id(fn)]
    cells = []
    for c in fn.__closure__:
        try:
            v = c.cell_contents
        except ValueError:
            cells.append(c)
            continue
        if isinstance(v, types.FunctionType) and v.__name__ in ("<lambda>", "fn", "f2"):
            v = _snap(v, depth + 1, memo)
        cells.append(types.CellType(v))
    g = types.FunctionType(fn.__code__, fn.__globals__, fn.__name__, fn.__defaults__, tuple(cells))
    g.__kwdefaults__ = fn.__kwdefaults__
    memo[id(fn)] = g
    return g


class Prog:
    def __init__(self, nc):
        self.nc = nc
        self.ops = []
        self.engs = {"pe": nc.tensor, "act": nc.scalar, "dve": nc.vector, "pool": nc.gpsimd, "sp": nc.sync}

    def add(self, eng, fn, reads=(), writes=(), dma_sem=None, ndma=0):
        op = Op(eng, _snap(fn))
        op.dma_sem = dma_sem
        op.ndma = ndma
        op.tag = "R:" + ",".join(r.name for r in reads[:3]) + " W:" + ",".join(w.name for w in writes[:3])
        deps = []
        for r in reads:
            if r.last_w is not None:
                deps.append(r.last_w)
        for w in writes:
            if w.last_w is not None:
                deps.append(w.last_w)
            deps.extend(w.readers)
        seen = set()
        for d in deps:
            if id(d) in seen or d is op:
                continue
            seen.add(id(d))
            if d.dma_sem is None and d.eng == "pe" and eng == "pe":
                continue
            op.deps.append(d)
            d.signal = True
        for r in reads:
            r.readers.append(op)
        for w in writes:
            w.last_w = op
            w.readers = []
        self.ops.append(op)
        return op

    def emit(self, es, final_waits):
        nc = self.nc
        sems = {e: es.enter_context(nc.semaphore("s_" + e)) for e in self.engs}
        cnt = {e: 0 for e in self.engs}
        dcnt = {}
        waited = {e: {} for e in self.engs}
        import os
        TRACE = os.environ.get("KTRACE")
        for oi, op in enumerate(self.ops):
            E = self.engs[op.eng]
            w = waited[op.eng]
            if TRACE and TRACE in op.tag:
                print("OP", oi, op.eng, op.tag, "signal", op.signal, "deps", [(d.eng, d.tag, d.sem.num if d.sem else None, d.val) for d in op.deps])
            for d in op.deps:
                key = d.sem.num
                if w.get(key, 0) >= d.val:
                    continue
                w[key] = d.val
                E.wait_ge(d.sem, d.val)
            if op.dma_sem is not None:
                op.sem = op.dma_sem
                op.fn(E, op.dma_sem)
                dcnt[op.dma_sem.num] = dcnt.get(op.dma_sem.num, 0) + 16 * op.ndma
                op.val = dcnt[op.dma_sem.num]
            else:
                ins = op.fn(E)
                if op.signal:
                    cnt[op.eng] += 1
                    op.sem = sems[op.eng]
                    op.val = cnt[op.eng]
                    ins.then_inc(sems[op.eng], 1)
        for s in final_waits:
            nc.sync.wait_ge(s, dcnt[s.num])


def build_nc(NSEQ, SEQ, dbg=(), dbg_tile=0):
    NT = SEQ // TT
    NB = SEQ // 128
    nc = bass.Bass("TRN2", target_bir_lowering=False)
    from contextlib import ExitStack
    es = ExitStack()

    def din(name, shape):
        return nc.dram_tensor(name, list(shape), F32, kind="ExternalInput").ap()

    x_d = din("x", [NSEQ, SEQ, D])
    c16_d = din("c16", [NSEQ * 8, 128])
    wada_d = din("w_ada", [D, 9 * D])
    pvec_d = din("pvec", [128, 128])
    gain_d = din("gain16", [16, 64])
    bf_d = din("b_forget", [8, 1])
    sink_d = din("sinks", [1, 8])
    w1g_d = din("w1g", [D, DFF]); w1u_d = din("w1u", [D, DFF]); w1d_d = din("w1d", [DFF, D])
    w2g_d = din("w2g", [D, DFF]); w2u_d = din("w2u", [D, DFF]); w2d_d = din("w2d", [DFF, D])
    win_d = din("w_in", [D, DIN])
    wout_d = din("w_out", [D, D])
    wf_d = din("w_f128", [D, 128])
    ident_d = din("ident", [128, 128])
    maskneg_d = din("maskneg", [128, 128])
    biasA_d = din("biasA", [4, 128, 512])
    sel8_d = din("sel8", [128, 8 * 128])
    sel64_d = din("sel64", [128, 64])
    shift_d = din("shiftI", [64, 128])
    out_d = nc.dram_tensor("out", [NSEQ, SEQ, D], F32, kind="ExternalOutput").ap()

    P = Prog(nc)
    with es:
        def sb(name, shape, dt=F32):
            return nc.alloc_sbuf_tensor("sb_" + name, list(shape), dt)

        def dsem(name):
            return es.enter_context(nc.semaphore(name))

        KTb = sb("KTb", [128, 4, SEQ], BF16)
        Vb = sb("Vb", [128, NB, 8 * 65], BF16)
        KTa = sb("KTa", [128, 5, 128], BF16)
        Va = sb("Va", [128, 5, 2 * 65], BF16)
        ncS = sb("ncS", [128, NB, 8])
        Bcols = sb("Bcols", [128, NB, 8])
        xa = sb("xa", [128, 8, TT])
        hin = sb("hin", [128, 8, TT], BF16)
        hT = sb("hT", [128, 28, TT], BF16)
        wgu = [sb("wgu%d" % i, [128, 2, 8, 256], BF16) for i in range(2)]
        wd = [sb("wd%d" % i, [128, NJ * 128], BF16) for i in range(2)]
        PT = [sb("PT%d" % i, [128, TT], BF16) for i in range(3)]
        sg = [sb("sg%d" % i, [128, TT]) for i in range(2)]
        tmpf = [sb("tmpf%d" % i, [128, TT]) for i in range(2)]
        prebf = [sb("prebf%d" % i, [128, TT], BF16) for i in range(2)]
        sqbf = [sb("sqbf%d" % i, [128, TT], BF16) for i in range(2)]
        Uev = [sb("Uev%d" % i, [128, TT]) for i in range(2)]
        ontmp = [sb("ontmp%d" % i, [64, TT]) for i in range(2)]
        mean_sb = sb("mean_sb", [128, TT])
        rstd_sb = sb("rstd_sb", [128, TT])
        rrA = sb("rrA", [128, TT])
        rrB = sb("rrB", [128, TT])
        xin = [sb("xin%d" % i, [128, D]) for i in range(2)]
        ident = sb("ident", [128, 128])
        identb = sb("identb", [128, 128], BF16)
        onesb = sb("onesb", [128, 128], BF16)
        onesf = sb("onesf", [8, 128])
        maskneg = sb("maskneg", [128, 128], BF16)
        biasA = sb("biasA", [128, 4, 512])
        sel8 = sb("sel8", [128, 8 * 128], BF16)
        sel64 = sb("sel64", [128, 64])
        sel64b = sb("sel64b", [128, 64], BF16)
        shiftI = sb("shiftI", [64, 128], BF16)
        pv = sb("pv", [128, 128])
        gainH = sb("gainH", [64, 16])
        gin = sb("gin", [16, 64])
        c16 = sb("c16", [NSEQ * 8, 128])
        scT = sb("scT", [128, NSEQ * 8])
        modsT = sb("modsT", [128, 72 * NSEQ])
        cst = sb("cst", [128, NSEQ, 16, 8])
        bfcol = sb("bfcol", [8, 1])
        esink = sb("esink", [65, 8])
        fsp = sb("fsp", [8, TT])
        ncT = sb("ncT", [8, TT])
        AUGT = sb("AUGT", [128, TT], BF16)
        ncar = [sb("ncar%d" % i, [8, 1]) for i in range(2)]
        dg8 = sb("dg8", [8, 8])
        actdummy = sb("actdummy", [1, 4])
        cbc = sb("cbc", [128, 8])
        ps = [nc.alloc_psum_tensor("ps%d" % i, [128, 512], F32) for i in range(8)]

        B = {}

        def b(name):
            if name not in B:
                B[name] = Buf(name)
            return B[name]

        bank = [b("bank%d" % i) for i in range(8)]

        def dma(eng, sem, pairs, reads=(), writes=()):
            def fn(E, s):
                for (o, i) in pairs:
                    E.dma_start(out=o, in_=i).then_inc(s, 16)
            return P.add(eng, fn, reads=reads, writes=writes, dma_sem=sem, ndma=len(pairs))

        def comp(eng, fn, reads=(), writes=()):
            return P.add(eng, fn, reads=reads, writes=writes)

        s_c = dsem("d_const")
        dma("sp", s_c, [(ident[:], ident_d), (biasA[:], biasA_d.rearrange("k p n -> p k n")),
                        (sel64[:], sel64_d), (hT[:, 16, 0:256].bitcast(F32), pvec_d), (gin[:], gain_d), (c16[:], c16_d),
                        (bfcol[:], bf_d), (esink[64:65, :], sink_d)],
            writes=[b("ident"), b("biasA"), b("sel64"), b("hT16"), b("gin"), b("c16"), b("bfcol"), b("esink")])
        s_c2 = dsem("d_const2")
        dma("pool", s_c2, [(identb[:], ident_d), (maskneg[:], maskneg_d), (sel8[:], sel8_d), (sel64b[:], sel64_d), (shiftI[:], shift_d)],
            writes=[b("identb"), b("maskneg"), b("sel8"), b("sel64b"), b("shiftI")])
        comp("dve", lambda E: E.memset(onesb[:], 1.0), writes=[b("onesb")])
        comp("dve", lambda E: E.memset(onesf[:], 1.0), writes=[b("onesf")])
        comp("dve", lambda E: E.memset(actdummy[:], 1.0), writes=[b("actdummy")])
        comp("pool", lambda E: E.memset(AUGT[:], 0.0), writes=[b("AUGT")])
        comp("pool", lambda E: E.memset(Uev[0][:], 0.0), writes=[b("Uev0")])
        comp("pool", lambda E: E.memset(Uev[1][:], 0.0), writes=[b("Uev1")])
        comp("dve", lambda E: E.memset(Vb[:], 1.0), writes=[b("Vb_all")])
        comp("dve", lambda E: E.memset(Va[:], 1.0), writes=[b("Va_all")])
        comp("act", lambda E: E.activation(out=esink[64:65, :], in_=esink[64:65, :], func=AF.Exp),
             reads=[b("esink")], writes=[b("esink")])
        comp("act", lambda E: E.mul(bfcol[:], bfcol[:], -1.0), reads=[b("bfcol")], writes=[b("bfcol")])

        comp("pe", lambda E: E.transpose(ps[0][:, 0:128], hT[:, 16, 0:256].bitcast(F32), ident[:]),
             reads=[b("hT16"), b("ident")], writes=[bank[0]])
        comp("dve", lambda E: E.tensor_copy(pv[:], ps[0][:, 0:128]), reads=[bank[0]], writes=[b("pv")])
        comp("pe", lambda E: E.transpose(ps[1][0:64, 0:16], gin[:], ident[0:16, 0:16]),
             reads=[b("gin"), b("ident")], writes=[bank[1]])
        comp("dve", lambda E: E.tensor_copy(gainH[:], ps[1][0:64, 0:16]), reads=[bank[1]], writes=[b("gainH")])
        comp("pe", lambda E: E.transpose(ps[2][:, 0:NSEQ * 8], c16[:], ident[0:NSEQ * 8, 0:NSEQ * 8]),
             reads=[b("c16"), b("ident")], writes=[bank[2]])
        comp("act", lambda E: E.activation(out=scT[:], in_=ps[2][:, 0:NSEQ * 8], func=AF.Silu),
             reads=[bank[2]], writes=[b("scT")])

        wa_f = [hT[:, 0:8, :].bitcast(F32), hT[:, 8:16, :].bitcast(F32)]
        wa_bufs = [[b("hT%d" % j) for j in range(0, 8)], [b("hT%d" % j) for j in range(8, 16)]]
        s_wa = [dsem("d_wa0"), dsem("d_wa1")]
        wada_v = wada_d.rearrange("(c p) n -> p c n", p=128)
        NG = 36
        for g in range(NG):
            sl = g % 2
            wv = wa_f[sl]
            dma("sp", s_wa[sl], [(wv, wada_v[:, :, g * 256:(g + 1) * 256])], writes=wa_bufs[sl])
            for jj in range(2):
                m8 = g * 2 + jj

                def fn(E, wv=wv, jj=jj, m8=m8):
                    for c in range(8):
                        rhs = scT[:, c:c + 8 * (NSEQ - 1) + 1:8] if NSEQ > 1 else scT[:, c:c + 1]
                        ins = E.matmul(ps[3][:, m8 * NSEQ:(m8 + 1) * NSEQ], wv[:, c, jj * 128:(jj + 1) * 128],
                                       rhs, start=(c == 0), stop=(c == 7), skip_group_check=True)
                    return ins
                comp("pe", fn, reads=wa_bufs[sl] + [b("scT")], writes=[bank[3]])
        comp("dve", lambda E: E.tensor_copy(modsT[:], ps[3][:, 0:72 * NSEQ]), reads=[bank[3]], writes=[b("modsT")])

        mv = modsT[:].rearrange("p (m c b) -> p b m c", m=9, c=8, b=NSEQ)

        def pvv(r):
            return pv[:, r * 8:(r + 1) * 8]
        BADA = lambda m: pvv(m)
        LNG = lambda i: pvv(9 + 2 * (i - 1))
        LNB = lambda i: pvv(10 + 2 * (i - 1))

        def cop(f, extra=()):
            comp("dve", f, reads=[b("cst"), b("modsT"), b("pv")] + list(extra), writes=[b("cst")])

        for bb in range(NSEQ):
            C = lambda k, bb=bb: cst[:, bb, k, :]
            M = lambda m, bb=bb: mv[:, bb, m, :]
            for (m, k) in [(0, 1), (1, 0), (2, 2), (3, 6), (4, 5), (5, 7), (6, 11), (7, 10), (8, 12)]:
                cop(lambda E, m=m, k=k: E.tensor_tensor(out=C(k), in0=M(m), in1=BADA(m), op=ALU.add))
            for k in (0, 5, 10):
                cop(lambda E, k=k: E.tensor_scalar_add(C(k), C(k), 1.0))
            cop(lambda E: E.tensor_scalar(out=C(2), in0=C(2), scalar1=1.0, scalar2=0.5, op0=ALU.add, op1=ALU.mult))
            cop(lambda E: E.tensor_scalar_add(C(7), C(7), 1.0))
            cop(lambda E: E.tensor_scalar(out=C(12), in0=C(12), scalar1=1.0, scalar2=0.5, op0=ALU.add, op1=ALU.mult))
            cop(lambda E: E.tensor_tensor(out=C(15), in0=LNB(1), in1=C(5), op=ALU.mult))
            cop(lambda E: E.tensor_tensor(out=C(6), in0=C(6), in1=C(15), op=ALU.add))
            cop(lambda E: E.tensor_tensor(out=C(5), in0=C(5), in1=LNG(1), op=ALU.mult))
            cop(lambda E: E.tensor_tensor(out=C(15), in0=LNB(2), in1=C(10), op=ALU.mult))
            cop(lambda E: E.tensor_tensor(out=C(11), in0=C(11), in1=C(15), op=ALU.add))
            cop(lambda E: E.tensor_tensor(out=C(10), in0=C(10), in1=LNG(2), op=ALU.mult))
            cop(lambda E: E.tensor_scalar_mul(C(0), C(0), 1.0 / ALPHA))
            cop(lambda E: E.tensor_scalar_mul(C(3), LNG(1), ALPHA))
            cop(lambda E: E.tensor_scalar_mul(C(4), LNB(1), ALPHA))
            cop(lambda E: E.tensor_scalar_mul(C(8), LNG(2), ALPHA))
            cop(lambda E: E.tensor_scalar_mul(C(9), LNB(2), ALPHA))
            cop(lambda E: E.tensor_copy(C(13), LNG(3)))
            cop(lambda E: E.tensor_copy(C(14), LNB(3)))

        dbg_sems = []

        def dump(tag, ap_fn, bufs, dt=F32):
            if tag not in dbg:
                return
            ap = ap_fn()
            dd = nc.dram_tensor("dbg_" + tag, list(ap.shape), dt, kind="ExternalOutput").ap()
            sm = dsem("d_dbg_" + tag)
            dbg_sems.append(sm)
            dma("sp", sm, [(dd, ap)], reads=bufs)

        dump("cst", lambda: cst[:], [b("cst")])
        dump("modsT", lambda: modsT[:], [b("modsT")])
        dump("pv", lambda: pv[:], [b("pv")])

        def CS(bb, k, c):
            return cst[:, bb, k, c:c + 1]

        s_wgu = [dsem("d_wgu0"), dsem("d_wgu1")]
        s_wd = [dsem("d_wd0"), dsem("d_wd1")]
        wctr = {"gu": 0, "d": 0}

        scr = {}
        s_st = {"gu0": dsem("d_stg0"), "gu1": dsem("d_stg1"), "d0": dsem("d_std0"), "d1": dsem("d_std1")}

        def load_gu(key, pieces_g, pieces_u):
            sl = wctr["gu"] % 2
            wctr["gu"] += 1
            img = wgu[sl][:].rearrange("p a c n -> p (a c n)")
            if key in scr:
                dma("sp", s_wgu[sl], [(img, scr[key])], reads=[b("scr_" + key)], writes=[b("wgu%d" % sl)])
                return sl
            pairs = []
            for half, pcs in ((0, pieces_g), (1, pieces_u)):
                for (src, off, n) in pcs:
                    pairs.append((wgu[sl][:, half, :, off:off + n], src))
            dma("pool", s_wgu[sl], pairs, writes=[b("wgu%d" % sl)])
            scr[key] = nc.dram_tensor("scr_" + key, [128, 2 * 8 * 256], BF16, kind="Internal").ap()
            dma("sp", s_st["gu%d" % sl], [(scr[key], img)], reads=[b("wgu%d" % sl)], writes=[b("scr_" + key)])
            return sl

        def load_d(key, pairs_fn, np_=128, nf=NJ * 128):
            sl = wctr["d"] % 2
            wctr["d"] += 1
            img = wd[sl][0:np_, 0:nf]
            if key in scr:
                dma("sp", s_wd[sl], [(img, scr[key])], reads=[b("scr_" + key)], writes=[b("wd%d" % sl)])
                return sl
            dma("pool", s_wd[sl], pairs_fn(wd[sl]), writes=[b("wd%d" % sl)])
            scr[key] = nc.dram_tensor("scr_" + key, [np_, nf], BF16, kind="Internal").ap()
            dma("sp", s_st["d%d" % sl], [(scr[key], img)], reads=[b("wd%d" % sl)], writes=[b("scr_" + key)])
            return sl

        st = {"pb": 0, "tn": 0}

        def residual_chunk(bb, n, ybanks, scal, gate_k, first, rr=None):
            G = CS(bb, gate_k, n)
            if rr is None:
                (yb,) = ybanks
                comp("dve", lambda E: E.scalar_tensor_tensor(out=xa[:, n, :], in0=ps[yb][:, :], scalar=G, in1=xa[:, n, :],
                                                             op0=ALU.mult, op1=ALU.add),
                     reads=[bank[yb], b("cst"), b("xa%d" % n)], writes=[b("xa%d" % n)])
            else:
                ya, yb2 = ybanks
                t = tmpf[st["tn"] % 2]; tb = b("tmpf%d" % (st["tn"] % 2)); st["tn"] += 1
                comp("dve", lambda E: E.tensor_tensor(out=t[:], in0=ps[ya][:, :], in1=rrA[:], op=ALU.mult),
                     reads=[bank[ya], b("rrA")], writes=[tb])
                t2 = tmpf[st["tn"] % 2]; tb2 = b("tmpf%d" % (st["tn"] % 2)); st["tn"] += 1
                comp("dve", lambda E: E.tensor_tensor(out=t2[:], in0=ps[yb2][:, :], in1=rrB[:], op=ALU.mult),
                     reads=[bank[yb2], b("rrB")], writes=[tb2])
                comp("dve", lambda E: E.tensor_tensor(out=t[:], in0=t[:], in1=t2[:], op=ALU.add),
                     reads=[tb, tb2], writes=[tb])
                comp("dve", lambda E: E.scalar_tensor_tensor(out=xa[:, n, :], in0=t[:], scalar=G, in1=xa[:, n, :],
                                                             op0=ALU.mult, op1=ALU.add),
                     reads=[tb, b("cst"), b("xa%d" % n)], writes=[b("xa%d" % n)])
            k = st["pb"] % 2; st["pb"] += 1
            comp("act", lambda E: E.activation(out=sqbf[k][:], in_=xa[:, n, :], func=AF.Square),
                 reads=[b("xa%d" % n)], writes=[b("sqbf%d" % k)])
            comp("dve", lambda E: E.tensor_copy(prebf[k][:], xa[:, n, :]),
                 reads=[b("xa%d" % n)], writes=[b("prebf%d" % k)])

            def fn(E):
                E.matmul(ps[6][:, :], onesb[:], prebf[k][:], start=first, stop=False, skip_group_check=True)
                return E.matmul(ps[7][:, :], onesb[:], sqbf[k][:], start=first, stop=False, skip_group_check=True)
            return ("pe", fn, [b("onesb"), b("prebf%d" % k), b("sqbf%d" % k)], [bank[6], bank[7]])

        def layernorm(bb, Rs_k, Rb_k, Hs_k, Hb_k, final=False):
            inv = 1.0 / D
            comp("dve", lambda E: E.tensor_scalar_mul(mean_sb[:], ps[6][:, :], inv), reads=[bank[6]], writes=[b("mean")])
            comp("dve", lambda E: E.tensor_tensor(out=rstd_sb[:], in0=mean_sb[:], in1=mean_sb[:], op=ALU.mult),
                 reads=[b("mean")], writes=[b("rstd")])
            comp("dve", lambda E: E.scalar_tensor_tensor(out=rstd_sb[:], in0=ps[7][:, :], scalar=inv, in1=rstd_sb[:],
                                                         op0=ALU.mult, op1=ALU.subtract),
                 reads=[bank[7], b("rstd")], writes=[b("rstd")])
            comp("act", lambda E: E.activation(out=rstd_sb[:], in_=rstd_sb[:], func=AF.Ln, bias=LN_EPS),
                 reads=[b("rstd")], writes=[b("rstd")])
            comp("act", lambda E: E.activation(out=rstd_sb[:], in_=rstd_sb[:], func=AF.Exp, scale=-0.5),
                 reads=[b("rstd")], writes=[b("rstd")])
            for c in range(8):
                t = tmpf[st["tn"] % 2]; tb = b("tmpf%d" % (st["tn"] % 2)); st["tn"] += 1
                comp("dve", lambda E, c=c, t=t: E.tensor_tensor(out=t[:], in0=xa[:, c, :], in1=mean_sb[:], op=ALU.subtract),
                     reads=[b("xa%d" % c), b("mean")], writes=[tb])
                comp("dve", lambda E, t=t: E.tensor_tensor(out=t[:], in0=t[:], in1=rstd_sb[:], op=ALU.mult),
                     reads=[tb, b("rstd")], writes=[tb])
                if not final:
                    comp("act", lambda E, c=c, t=t: E.activation(out=hin[:, c, :], in_=t[:], func=AF.Identity,
                                                                 scale=CS(bb, Hs_k, c), bias=CS(bb, Hb_k, c)),
                         reads=[tb, b("cst")], writes=[b("hin%d" % c)])
                comp("act", lambda E, c=c, t=t: E.activation(out=xa[:, c, :], in_=t[:], func=AF.Identity,
                                                             scale=CS(bb, Rs_k, c), bias=CS(bb, Rb_k, c)),
                     reads=[tb, b("cst")], writes=[b("xa%d" % c)])

        def ffn(fid, bb, wg_d, wu_d, wdn_d, gate_k, Rs_k, Rb_k, Hs_k, Hb_k, final):
            wg_v = wg_d.rearrange("(c p) n -> p c n", p=128)
            wu_v = wu_d.rearrange("(c p) n -> p c n", p=128)
            wdn_v = wdn_d.rearrange("(j p) n -> p j n", p=128)
            hinb = [b("hin%d" % c) for c in range(8)]
            for slab in range(NJ // 2):
                c0 = slab * 256
                sl = load_gu("f%dgu%d" % (fid, slab), [(wg_v[:, :, c0:c0 + 256], 0, 256)], [(wu_v[:, :, c0:c0 + 256], 0, 256)])
                for jj in range(2):
                    j = slab * 2 + jj
                    gb = j % 2
                    ub = 2 + j % 2

                    def fn(E, sl=sl, jj=jj, gb=gb, ub=ub):
                        for c in range(8):
                            E.matmul(ps[gb][:, :], wgu[sl][:, 0, c, jj * 128:(jj + 1) * 128], hin[:, c, :],
                                     start=(c == 0), stop=(c == 7))
                        for c in range(8):
                            ins = E.matmul(ps[ub][:, :], wgu[sl][:, 1, c, jj * 128:(jj + 1) * 128], hin[:, c, :],
                                           start=(c == 0), stop=(c == 7))
                        return ins
                    comp("pe", fn, reads=[b("wgu%d" % sl)] + hinb, writes=[bank[gb], bank[ub]])
                    k = j % 2
                    comp("act", lambda E, gb=gb, k=k: E.activation(out=sg[k][:], in_=ps[gb][:, :], func=AF.Silu),
                         reads=[bank[gb]], writes=[b("sg%d" % k)])
                    comp("dve", lambda E, ub=ub, k=k, j=j: E.tensor_tensor(out=hT[:, j, :], in0=sg[k][:], in1=ps[ub][:, :],
                                                                          op=ALU.mult),
                         reads=[bank[ub], b("sg%d" % k)], writes=[b("hT%d" % j)])
            hTb = [b("hT%d" % j) for j in range(NJ)]
            pend = None
            for n in range(8):
                sl = load_d("f%dd%d" % (fid, n), lambda w, n=n: [
                    (w.rearrange("p (j n) -> p j n", j=NJ)[:, 0:11, :], wdn_v[:, 0:11, n * 128:(n + 1) * 128]),
                    (w.rearrange("p (j n) -> p j n", j=NJ)[:, 11:22, :], wdn_v[:, 11:22, n * 128:(n + 1) * 128])])
                yb = 4 + n % 2

                def fn(E, sl=sl, yb=yb):
                    wv = wd[sl].rearrange("p (j n) -> p j n", j=NJ)
                    for j in range(NJ):
                        ins = E.matmul(ps[yb][:, :], wv[:, j, :], hT[:, j, :], start=(j == 0), stop=(j == NJ - 1))
                    return ins
                comp("pe", fn, reads=[b("wd%d" % sl)] + hTb, writes=[bank[yb]])
                if pend is not None:
                    comp(pend[0], pend[1], reads=pend[2], writes=pend[3])
                pend = residual_chunk(bb, n, (yb,), None, gate_k, first=(n == 0))
            comp(pend[0], pend[1], reads=pend[2], writes=pend[3])
            layernorm(bb, Rs_k, Rb_k, Hs_k, Hb_k, final=final)

        s_x = [dsem("d_x0"), dsem("d_x1")]
        s_o = [dsem("d_o0"), dsem("d_o1")]
        xctr = {"n": 0}
        preloaded = set()
        win_v = win_d.rearrange("(c p) n -> p c n", p=128)
        wout_v = wout_d.rearrange("(h p) n -> p h n", p=64)
        hinb = [b("hin%d" % c) for c in range(8)]

        for bb in range(NSEQ):
            for it in range(NT):
                t0 = it * TT
                for u in range(4):
                    k = u % 2
                    if (bb, it, u) not in preloaded:
                        dma("sp", s_x[k], [(xin[k][:], x_d[bb, t0 + u * 128:t0 + (u + 1) * 128, :])],
                            writes=[b("xin%d" % k)])
                    for half in range(2):
                        pb = half

                        def fn(E, k=k, half=half, pb=pb):
                            for cc in range(4):
                                c = half * 4 + cc
                                ins = E.transpose(ps[pb][:, cc * 128:(cc + 1) * 128], xin[k][:, c * 128:(c + 1) * 128], ident[:])
                            return ins
                        comp("pe", fn, reads=[b("xin%d" % k), b("ident")], writes=[bank[pb]])
                        eng = "act" if half == 0 else "dve"
                        if eng == "act":
                            f2 = lambda E, half=half, pb=pb, u=u: E.mul(
                                xa[:, half * 4:half * 4 + 4, u * 128:(u + 1) * 128],
                                ps[pb][:, :].rearrange("p (c n) -> p c n", c=4), ALPHA)
                        else:
                            f2 = lambda E, half=half, pb=pb, u=u: E.tensor_scalar_mul(
                                xa[:, half * 4:half * 4 + 4, u * 128:(u + 1) * 128],
                                ps[pb][:, :].rearrange("p (c n) -> p c n", c=4), ALPHA)
                        comp(eng, f2, reads=[bank[pb]], writes=[b("xa%d" % c) for c in range(half * 4, half * 4 + 4)])
                for c in range(8):
                    comp("act", lambda E, c=c: E.activation(out=hin[:, c, :], in_=xa[:, c, :], func=AF.Identity,
                                                            scale=CS(bb, 0, c), bias=CS(bb, 1, c)),
                         reads=[b("xa%d" % c), b("cst")], writes=[b("hin%d" % c)])

                DB = (bb == 0 and it == dbg_tile)
                xab = [b("xa%d" % c) for c in range(8)]
                if DB:
                    dump("xa0", lambda: xa[:], xab)
                    dump("hin0", lambda: hin[:], hinb, BF16)
                ffn(1, bb, w1g_d, w1u_d, w1d_d, 2, 3, 4, 5, 6, final=False)
                if DB:
                    dump("hT1", lambda: hT[:, 0:22, :], [b("hT%d" % j) for j in range(22)], BF16)
                    dump("xa1", lambda: xa[:], xab)
                    dump("hin1", lambda: hin[:], hinb, BF16)

                QTa = lambda cp: hT[:, 16 + cp, :]
                QTb = lambda cp: hT[:, 20 + cp, :]
                def proj_fm(sl, half, coff, dst_fn, dst_bufs, scale, pbk, eng="act", M=128, qpair=None):
                    def fn(E):
                        for c in range(8):
                            ins = E.matmul(ps[pbk][0:M, :], wgu[sl][:, half, c, coff:coff + M], hin[:, c, :],
                                           start=(c == 0), stop=(c == 7))
                        return ins
                    comp("pe", fn, reads=[b("wgu%d" % sl)] + hinb, writes=[bank[pbk]])
                    if qpair is not None:
                        h0 = 2 * qpair
                        comp("act", lambda E: E.mul(hT[0:64, 20 + h0, :], ps[pbk][0:64, :], scale),
                             reads=[bank[pbk]], writes=[b("hT%d" % (20 + h0))])
                        comp("dve", lambda E: E.tensor_scalar_mul(hT[64:128, 21 + h0, :], ps[pbk][64:128, :], scale),
                             reads=[bank[pbk]], writes=[b("hT%d" % (21 + h0))])
                        comp("pool", lambda E: E.memset(hT[64:128, 20 + h0, :], 0.0), writes=[b("hT%d" % (20 + h0))])
                        comp("pool", lambda E: E.memset(hT[0:64, 21 + h0, :], 0.0), writes=[b("hT%d" % (21 + h0))])
                        return
                    if eng == "act":
                        comp("act", lambda E: E.mul(dst_fn(), ps[pbk][0:M, :], scale), reads=[bank[pbk]], writes=dst_bufs)
                    else:
                        comp("dve", lambda E: E.tensor_scalar_mul(dst_fn(), ps[pbk][0:M, :], scale), reads=[bank[pbk]], writes=dst_bufs)

                pg = []
                for cp in range(2):
                    pg += [(win_v[:, :, cp * 64:(cp + 1) * 64], cp * 128, 64),
                           (win_v[:, :, (cp + 4) * 64:(cp + 5) * 64], cp * 128 + 64, 64)]
                pu = []
                for cp in range(2, 4):
                    pu += [(win_v[:, :, cp * 64:(cp + 1) * 64], (cp - 2) * 128, 64),
                           (win_v[:, :, (cp + 4) * 64:(cp + 5) * 64], (cp - 2) * 128 + 64, 64)]
                sl = load_gu("winA", pg, pu)
                for cp in range(4):
                    proj_fm(sl, cp // 2, (cp % 2) * 128, lambda cp=cp: QTa(cp), [b("hT%d" % (16 + cp))], 0.125,
                            cp % 4, eng=("act" if cp % 2 == 0 else "dve"))
                sl = load_gu("winB", [(win_v[:, :, 512:768], 0, 256)], [(win_v[:, :, 768:1024], 0, 256)])
                proj_fm(sl, 0, 0, lambda: KTa[:, 1:5, :].rearrange("p a n -> p (a n)"),
                        [b("KTa%d" % a) for a in range(1, 5)], 1.0, 0, eng="dve")
                for u in range(4):
                    pbk = 4 + u % 2

                    def fn(E, sl=sl, u=u, pbk=pbk):
                        for c in range(8):
                            ins = E.matmul(ps[pbk][:, 0:128], hin[:, c, u * 128:(u + 1) * 128], wgu[sl][:, 0, c, 128:256],
                                           start=(c == 0), stop=(c == 7))
                        return ins
                    comp("pe", fn, reads=[b("wgu%d" % sl)] + hinb, writes=[bank[pbk]])
                    comp("act", lambda E, u=u, pbk=pbk: E.copy(
                        Va[:, 1 + u, :].rearrange("p (k e) -> p k e", k=2)[:, :, 0:64],
                        ps[pbk][:, 0:128].rearrange("p (k e) -> p k e", k=2)),
                        reads=[bank[pbk], b("Va_all")], writes=[b("Va%d" % (1 + u))])
                for cp in range(2):
                    proj_fm(sl, 1, cp * 128, None, None, 0.125, 1 + cp, qpair=cp)
                sl = load_gu("winC", [(win_v[:, :, 1024:1280], 0, 256)], [(win_v[:, :, 1280:1536], 0, 256)])
                for cp in range(2, 4):
                    proj_fm(sl, 0, (cp - 2) * 128, None, None, 0.125, cp, qpair=cp)
                for cp in range(2):
                    proj_fm(sl, 1, cp * 128, lambda cp=cp: KTb[:, cp, t0:t0 + TT], [b("KTb%d" % cp)], 1.0, cp,
                            eng=("act" if cp == 0 else "dve"))
                sl = load_gu("winD", [(win_v[:, :, 1536:1792], 0, 256)], [(win_v[:, :, 1792:2048], 0, 256)])
                for cp in range(2, 4):
                    proj_fm(sl, 0, (cp - 2) * 128, lambda cp=cp: KTb[:, cp, t0:t0 + TT], [b("KTb%d" % cp)], 1.0, cp,
                            eng=("act" if cp == 2 else "dve"))
                slD = sl
                sl = load_gu("winE", [(win_v[:, :, 2048:2304], 0, 256)], [(wf_d.rearrange("(c p) n -> p c n", p=128), 0, 128)])
                slE = sl
                for u in range(4):
                    pbk = 4 + u % 2

                    def fn(E, u=u, pbk=pbk):
                        for c in range(8):
                            E.matmul(ps[pbk][:, 0:256], hin[:, c, u * 128:(u + 1) * 128], wgu[slD][:, 1, c, 0:256],
                                     start=(c == 0), stop=(c == 7))
                        for c in range(8):
                            ins = E.matmul(ps[pbk][:, 256:512], hin[:, c, u * 128:(u + 1) * 128], wgu[slE][:, 0, c, 0:256],
                                           start=(c == 0), stop=(c == 7), skip_group_check=True)
                        return ins
                    comp("pe", fn, reads=[b("wgu%d" % slD), b("wgu%d" % slE)] + hinb, writes=[bank[pbk]])
                    blk = it * 4 + u
                    comp("act" if u % 2 == 0 else "dve",
                         (lambda E, blk=blk, pbk=pbk: E.copy(
                             Vb[:, blk, :].rearrange("p (h e) -> p h e", h=8)[:, :, 0:64],
                             ps[pbk][:, :].rearrange("p (h e) -> p h e", h=8))) if u % 2 == 0 else
                         (lambda E, blk=blk, pbk=pbk: E.tensor_copy(
                             Vb[:, blk, :].rearrange("p (h e) -> p h e", h=8)[:, :, 0:64],
                             ps[pbk][:, :].rearrange("p (h e) -> p h e", h=8))),
                         reads=[bank[pbk], b("Vb_all")], writes=[b("Vb%d" % blk)])
                def fn(E):
                    for c in range(8):
                        ins = E.matmul(ps[6][:, :], wgu[slE][:, 1, c, 0:128], hin[:, c, :], start=(c == 0), stop=(c == 7))
                    return ins
                comp("pe", fn, reads=[b("wgu%d" % slE)] + hinb, writes=[bank[6]])
                if "rawf" in dbg:
                    comp("act", lambda E: E.copy(fsp[:], ps[6][0:8, :]), reads=[bank[6], b("bfcol")], writes=[b("fsp")])
                else:
                    comp("act", lambda E: E.activation(out=fsp[:], in_=ps[6][0:8, :], func=AF.Exp, scale=-1.0, bias=bfcol[:]),
                         reads=[bank[6], b("bfcol")], writes=[b("fsp")])
                    comp("act", lambda E: E.activation(out=fsp[:], in_=fsp[:], func=AF.Ln, bias=1.0),
                         reads=[b("fsp")], writes=[b("fsp")])
                cur = ncar[it % 2]; prv = ncar[(it + 1) % 2]
                curb = b("ncar%d" % (it % 2)); prvb = b("ncar%d" % ((it + 1) % 2))
                if it == 0:
                    comp("dve", lambda E: E.memset(prv[:], 0.0), writes=[prvb])
                comp("dve", lambda E: E.tensor_tensor_scan(out=ncT[:], data0=fsp[:], data1=fsp[:], initial=prv[:],
                                                           op0=ALU.add, op1=ALU.max),
                     reads=[b("fsp"), prvb], writes=[b("ncT")])
                comp("dve", lambda E: E.tensor_copy(cur[:], ncT[:, TT - 1:TT]), reads=[b("ncT")], writes=[curb])
                comp("dve", lambda E: E.tensor_scalar(out=AUGT[0:8, :], in0=ncT[:], scalar1=prv[:], scalar2=-1.0,
                                                      op0=ALU.subtract, op1=ALU.mult),
                     reads=[b("ncT"), prvb], writes=[b("AUGT")])
                def fn(E):
                    for u in range(4):
                        ins = E.matmul(ps[7][:, u * 8:(u + 1) * 8], ncT[:, u * 128:(u + 1) * 128], ident[0:8, 0:8],
                                       start=True, stop=True, skip_group_check=True)
                    return ins
                comp("pe", fn, reads=[b("ncT"), b("ident")], writes=[bank[7]])
                comp("dve", lambda E: E.tensor_copy(ncS[:, it * 4:it * 4 + 4, :],
                                                    ps[7][:, 0:32].rearrange("p (u h) -> p u h", u=4)),
                     reads=[bank[7]], writes=[b("ncS")])
                comp("dve", lambda E: E.tensor_scalar_mul(dg8[:], ident[0:8, 0:8], prv[:]),
                     reads=[b("ident"), prvb], writes=[b("dg8")])
                comp("pe", lambda E: E.matmul(ps[6][:, 0:8], onesf[:], dg8[:], start=True, stop=True),
                     reads=[b("onesf"), b("dg8")], writes=[bank[6]])
                comp("dve", lambda E: E.tensor_copy(cbc[:], ps[6][:, 0:8]), reads=[bank[6]], writes=[b("cbc")])
                nblk = it * 4 + 4
                comp("dve", lambda E: E.tensor_tensor(out=Bcols[:, 0:nblk, :], in0=ncS[:, 0:nblk, :],
                                                      in1=cbc[:].unsqueeze(1).to_broadcast([128, nblk, 8]), op=ALU.subtract),
                     reads=[b("ncS"), b("cbc")], writes=[b("Bcols")])

                hctr = {"u": 0, "p": 0, "t": 0, "o": 0, "hl": 0}

                def finish_steps(obank, sink_kv, ssq_fn, gain_fn):
                    k = hctr["u"] % 2; hctr["u"] += 1
                    U = Uev[k]; Ub = b("Uev%d" % k)
                    comp("act", lambda E: E.copy(U[0:64, :], ps[obank][0:64, :]), reads=[bank[obank]], writes=[Ub])
                    if sink_kv is not None:
                        kv = sink_kv
                        comp("dve", lambda E: E.tensor_tensor(
                            out=U[64:65, :].rearrange("p (g q) -> p g q", g=4),
                            in0=ps[obank][64:65, :].rearrange("p (g q) -> p g q", g=4),
                            in1=esink[64:65, kv * 4:kv * 4 + 4].unsqueeze(2).to_broadcast([1, 4, 128]), op=ALU.add),
                            reads=[bank[obank], b("esink")], writes=[b("Urow%d" % k)])
                        comp("act", lambda E: E.activation(out=U[64:65, :], in_=U[64:65, :], func=AF.Ln),
                             reads=[b("Urow%d" % k)], writes=[b("Urow%d" % k)])
                    else:
                        comp("act", lambda E: E.activation(out=U[64:65, :], in_=ps[obank][64:65, :], func=AF.Ln),
                             reads=[bank[obank]], writes=[b("Urow%d" % k)])
                    comp("act", lambda E: E.activation(out=U[64:65, :], in_=U[64:65, :], func=AF.Exp, scale=-1.0),
                         reads=[b("Urow%d" % k)], writes=[b("Urow%d" % k)])
                    hl = hctr["hl"] % 4; hctr["hl"] += 1
                    hiT = hin[:, 2 * hl, :]; loT = hin[:, 2 * hl + 1, :]
                    hib = b("hin%d" % (2 * hl)); lob = b("hin%d" % (2 * hl + 1))
                    comp("dve", lambda E: E.tensor_copy(hiT[64:65, :], U[64:65, :]),
                         reads=[b("Urow%d" % k)], writes=[hib])
                    comp("dve", lambda E: E.tensor_tensor(out=loT[64:65, :], in0=U[64:65, :], in1=hiT[64:65, :],
                                                          op=ALU.subtract),
                         reads=[b("Urow%d" % k), hib], writes=[lob])
                    rb = obank
                    hctr["p"] += 1
                    k2 = hctr["o"] % 2; hctr["o"] += 1
                    on = ontmp[k2]; onb = b("ontmp%d" % k2)
                    kq = st["pb"] % 2; st["pb"] += 1

                    def step1():
                        def fnb(E):
                            E.matmul(ps[rb][0:64, :], sel64b[:], hiT, start=True, stop=False)
                            return E.matmul(ps[rb][0:64, :], sel64b[:], loT, start=False, stop=True)
                        comp("pe", fnb, reads=[b("sel64b"), hib, lob], writes=[bank[rb]])
                        comp("dve", lambda E: E.tensor_tensor(out=on[:], in0=U[0:64, :], in1=ps[rb][0:64, :], op=ALU.mult),
                             reads=[Ub, bank[rb]], writes=[onb])
                        comp("act", lambda E: E.activation(out=sqbf[kq][0:64, :], in_=on[:], func=AF.Square),
                             reads=[onb], writes=[b("sqbf%d" % kq)])
                        gain_fn(on, onb)

                    def step2():
                        ssq_fn(kq)
                    return [step1, step2]

                sblocks = []
                for u in range(4):
                    nblk_q = it * 4 + u
                    for kv in range(2):
                        kbs = ([(u, 0)] if nblk_q > 0 else []) + [(u + 1, 1)]
                        for ik, (slot, kind) in enumerate(kbs):
                            sblocks.append(("s", u, kv, ik, len(kbs), slot, kind))
                nkb = it * 4 + 4
                fblocks = [("f", h, kb) for h in range(8) for kb in range(nkb)]
                nF, nS = len(fblocks), len(sblocks)
                blocks = []
                si = 0
                for fi, fb in enumerate(fblocks):
                    while si < nS and si * (nF - 8) <= fi * nS:
                        blocks.append(sblocks[si]); si += 1
                    blocks.append(fb)
                while si < nS:
                    blocks.append(sblocks[si]); si += 1
                SB = [0, 1, 4]
                LOOK = 2
                hctr["fox"] = True
                deferred = []
                fin = {"g": 0}

                def flush_upto(g):
                    while deferred and deferred[0][2] <= g:
                        deferred.pop(0)[1]()

                def flush_bank(ob):
                    last = -1
                    for k_, e in enumerate(deferred):
                        if e[3] == ob:
                            last = k_
                    for _ in range(last + 1):
                        deferred.pop(0)[1]()

                def emit_qk(i):
                    blk = blocks[i]
                    sbk = SB[i % 3]
                    pk = i % 3
                    if blk[0] == "f":
                        _, h, kb = blk
                        cp = h // 2
                        r = kb - it * 4
                        c0 = 0 if r < 0 else r * 128

                        def fn(E):
                            ins = E.matmul(ps[sbk][:, c0:TT], KTb[:, cp, kb * 128:(kb + 1) * 128],
                                           hT[:, 20 + h, c0:TT], start=True, stop=False, skip_group_check=True)
                            ins = E.matmul(ps[sbk][:, c0:TT], sel8[:, h * 128:(h + 1) * 128], AUGT[:, c0:TT],
                                           start=False, stop=(r < 0), skip_group_check=True)
                            if r >= 0:
                                ins = E.matmul(ps[sbk][:, c0:c0 + 128], identb[:], maskneg[:], start=False, stop=True,
                                               skip_group_check=True)
                            return ins
                        comp("pe", fn, reads=[b("KTb%d" % cp), b("hT%d" % (20 + h)), b("sel8"), b("AUGT"),
                                              b("identb"), b("maskneg")], writes=[bank[sbk]])
                        comp("act", lambda E: E.activation(
                            out=PT[pk][:, c0:TT], in_=ps[sbk][:, c0:TT], func=AF.Exp, bias=Bcols[:, kb, h:h + 1]),
                            reads=[bank[sbk], b("Bcols")], writes=[b("PT%d" % pk)])
                    else:
                        _, u, kv, ik, nk, slot, kind = blk
                        comp("pe", lambda E: E.matmul(
                            ps[sbk][:, :].rearrange("p (g q) -> p g q", g=4),
                            KTa[kv * 64:(kv + 1) * 64, slot, :],
                            hT[kv * 64:(kv + 1) * 64, 16:20, u * 128:(u + 1) * 128], start=True, stop=True),
                            reads=[b("KTa%d" % slot)] + [b("hT%d" % (16 + g)) for g in range(4)], writes=[bank[sbk]])
                        tk = st["tn"] % 2; st["tn"] += 1
                        t = tmpf[tk]; tb = b("tmpf%d" % tk)
                        comp("dve", lambda E: E.tensor_tensor(
                            out=t[:], in0=ps[sbk][:, :], in1=biasA[:, kv * 2 + kind, :], op=ALU.add),
                            reads=[bank[sbk], b("biasA")], writes=[tb])
                        comp("act", lambda E: E.activation(out=PT[pk][:], in_=t[:], func=AF.Exp),
                             reads=[tb], writes=[b("PT%d" % pk)])

                def emit_pv(i):
                    blk = blocks[i]
                    pk = i % 3
                    if blk[0] == "f":
                        _, h, kb = blk
                        r = kb - it * 4
                        c0 = 0 if r < 0 else r * 128
                        ob = 2 + h % 2
                        if kb == 0:
                            flush_bank(ob)
                        comp("pe", lambda E: E.matmul(
                            ps[ob][0:65, c0:TT], Vb[:, kb, h * 65:(h + 1) * 65], PT[pk][:, c0:TT],
                            start=(kb == 0), stop=(kb == nkb - 1), skip_group_check=True),
                            reads=[b("Vb%d" % kb), b("Vb_all"), b("PT%d" % pk)], writes=[bank[ob]])
                        if kb == nkb - 1:
                            def ssq_fn(kq, h=h):
                                comp("pe", lambda E: E.matmul(ps[7][:, :], onesb[0:64, :], sqbf[kq][0:64, :],
                                                              start=(h == 0), stop=(h == 7), skip_group_check=True),
                                     reads=[b("onesb"), b("sqbf%d" % kq)], writes=[bank[7]])

                            def gain_fn(on, onb, h=h):
                                comp("dve", lambda E: E.tensor_scalar_mul(hT[0:64, 8 + h, :], on[:], gainH[:, 8 + h:9 + h]),
                                     reads=[onb, b("gainH")], writes=[b("hT%d" % (8 + h))])
                            g = fin["g"]; fin["g"] += 1
                            flush_upto(g - 2)
                            steps = finish_steps(ob, None, ssq_fn, gain_fn)
                            deferred.append((i + LOOK + 5, steps[0], g, ob))
                            deferred.append((i + LOOK + 9, steps[1], g, -1))
                    else:
                        _, u, kv, ik, nk, slot, kind = blk
                        ob = 5
                        if ik == 0:
                            flush_bank(ob)
                        comp("pe", lambda E: E.matmul(
                            ps[ob][0:65, :], Va[:, slot, kv * 65:(kv + 1) * 65], PT[pk][:],
                            start=(ik == 0), stop=(ik == nk - 1)),
                            reads=[b("Va%d" % slot), b("Va_all"), b("PT%d" % pk)], writes=[bank[ob]])
                        if ik == nk - 1:
                            first = (u == 0 and kv == 0)

                            def ssq_fn(kq):
                                def fn(E):
                                    for g_ in range(4):
                                        ins = E.matmul(ps[6][:, u * 128:(u + 1) * 128], onesb[0:64, :],
                                                       sqbf[kq][0:64, g_ * 128:(g_ + 1) * 128],
                                                       start=(first and g_ == 0), stop=False, skip_group_check=True)
                                    return ins
                                comp("pe", fn, reads=[b("onesb"), b("sqbf%d" % kq)], writes=[bank[6]])

                            def gain_fn(on, onb):
                                for g_ in range(4):
                                    hs = kv * 4 + g_
                                    comp("dve", lambda E, g_=g_, hs=hs: E.tensor_scalar_mul(
                                        hT[0:64, hs, u * 128:(u + 1) * 128], on[:, g_ * 128:(g_ + 1) * 128], gainH[:, hs:hs + 1]),
                                        reads=[onb, b("gainH")], writes=[b("hT%d" % hs)])
                            g = fin["g"]; fin["g"] += 1
                            flush_upto(g - 2)
                            steps = finish_steps(ob, kv, ssq_fn, gain_fn)
                            deferred.append((i + LOOK + 5, steps[0], g, ob))
                            deferred.append((i + LOOK + 9, steps[1], g, -1))

                nb_ = len(blocks)
                for i in range(nb_ + LOOK):
                    if i < nb_:
                        emit_qk(i)
                    while deferred and deferred[0][0] <= i:
                        deferred.pop(0)[1]()
                    if i - LOOK >= 0:
                        emit_pv(i - LOOK)
                while deferred:
                    deferred.pop(0)[1]()
                hctr["fox"] = False
                comp("pool", lambda E: E.tensor_copy(KTa[:, 0, :], KTa[:, 4, :]), reads=[b("KTa4")], writes=[b("KTa0")])
                comp("pool", lambda E: E.tensor_copy(Va[:, 0, :], Va[:, 4, :]), reads=[b("Va4"), b("Va_all")], writes=[b("Va0")])
                for (rr_, bk_, nm_) in ((rrA, 6, "rrA"), (rrB, 7, "rrB")):
                    comp("dve", lambda E, rr_=rr_, bk_=bk_: E.tensor_scalar(out=rr_[:], in0=ps[bk_][:, :], scalar1=1.0 / 512,
                                                                           scalar2=RMS_EPS, op0=ALU.mult, op1=ALU.add),
                         reads=[bank[bk_]], writes=[b(nm_)])
                    comp("act", lambda E, rr_=rr_: E.activation(out=rr_[:], in_=rr_[:], func=AF.Ln), reads=[b(nm_)], writes=[b(nm_)])
                    comp("act", lambda E, rr_=rr_: E.activation(out=rr_[:], in_=rr_[:], func=AF.Exp, scale=-0.5),
                         reads=[b(nm_)], writes=[b(nm_)])

                if DB:
                    dump("att", lambda: hT[:, :, :], [b("hT%d" % j) for j in range(28)], BF16)
                    dump("KTa", lambda: KTa[:], [b("KTa%d" % a) for a in range(5)], BF16)
                    dump("Va", lambda: Va[:], [b("Va%d" % a) for a in range(5)] + [b("Va_all")], BF16)
                    dump("fsp", lambda: fsp[:], [b("fsp")])
                    dump("ncT", lambda: ncT[:], [b("ncT")])
                    dump("AUGT", lambda: AUGT[:], [b("AUGT")], BF16)
                    dump("Bcols", lambda: Bcols[:], [b("Bcols")])
                    dump("ncS", lambda: ncS[:], [b("ncS")])
                    dump("rrA", lambda: rrA[:], [b("rrA")])
                    dump("rrB", lambda: rrB[:], [b("rrB")])
                for hs in range(1, 16, 2):
                    pbk = (hs // 2) % 2
                    comp("pe", lambda E, hs=hs, pbk=pbk: E.matmul(ps[pbk][:, :], shiftI[:], hT[0:64, hs, :], start=True, stop=True),
                         reads=[b("shiftI"), b("hT%d" % hs)], writes=[bank[pbk]])
                    if pbk == 0:
                        comp("act", lambda E, hs=hs, pbk=pbk: E.copy(hT[64:128, hs - 1, :], ps[pbk][64:128, :]),
                             reads=[bank[pbk]], writes=[b("hT%d" % (hs - 1))])
                    else:
                        comp("dve", lambda E, hs=hs, pbk=pbk: E.tensor_copy(hT[64:128, hs - 1, :], ps[pbk][64:128, :]),
                             reads=[bank[pbk]], writes=[b("hT%d" % (hs - 1))])
                pend = None
                oTb = [b("hT%d" % hs) for hs in range(0, 16, 2)]
                wout_v2 = wout_d.rearrange("(c p) n -> p c n", p=128)
                for n in range(8):
                    sl = load_d("wo%d" % n, lambda w, n=n: [
                        (w[:, 0:1024].rearrange("p (c n) -> p c n", c=8), wout_v2[:, :, n * 128:(n + 1) * 128])], np_=128, nf=1024)
                    ya, yb2 = (0, 1) if n % 2 == 0 else (2, 3)

                    def fn(E, sl=sl, ya=ya, yb2=yb2):
                        wv = wd[sl][:, 0:1024].rearrange("p (c n) -> p c n", c=8)
                        for pr in range(4):
                            E.matmul(ps[ya][:, :], wv[:, pr, :], hT[:, 2 * pr, :], start=(pr == 0), stop=(pr == 3))
                        for pr in range(4, 8):
                            ins = E.matmul(ps[yb2][:, :], wv[:, pr, :], hT[:, 2 * pr, :], start=(pr == 4), stop=(pr == 7))
                        return ins
                    comp("pe", fn, reads=[b("wd%d" % sl)] + oTb, writes=[bank[ya], bank[yb2]])
                    if pend is not None:
                        comp(pend[0], pend[1], reads=pend[2], writes=pend[3])
                    pend = residual_chunk(bb, n, (ya, yb2), None, 7, first=(n == 0), rr=True)
                comp(pend[0], pend[1], reads=pend[2], writes=pend[3])
                layernorm(bb, 8, 9, 10, 11, final=False)

                if DB:
                    dump("xa2", lambda: xa[:], xab)
                ffn(2, bb, w2g_d, w2u_d, w2d_d, 12, 13, 14, None, None, final=True)

                nxt = (bb, it + 1) if it + 1 < NT else ((bb + 1, 0) if bb + 1 < NSEQ else None)
                if nxt is not None:
                    nb2, ni2 = nxt
                    for u in range(2):
                        dma("sp", s_x[u], [(xin[u][:], x_d[nb2, ni2 * TT + u * 128:ni2 * TT + (u + 1) * 128, :])],
                            writes=[b("xin%d" % u)])
                        preloaded.add((nb2, ni2, u))
                for u in range(4):
                    for half in range(2):
                        pb = half

                        def fn(E, u=u, half=half, pb=pb):
                            for cc in range(4):
                                c = half * 4 + cc
                                ins = E.transpose(ps[pb][:, cc * 128:(cc + 1) * 128], xa[:, c, u * 128:(u + 1) * 128], ident[:])
                            return ins
                        comp("pe", fn, reads=[b("xa%d" % c) for c in range(half * 4, half * 4 + 4)] + [b("ident")], writes=[bank[pb]])
                        if half == 0:
                            comp("act", lambda E, pb=pb: E.copy(tmpf[0][:], ps[pb][:, :]),
                                 reads=[bank[pb]], writes=[b("tmpf0")])
                        else:
                            comp("dve", lambda E, pb=pb: E.tensor_copy(tmpf[1][:], ps[pb][:, :]),
                                 reads=[bank[pb]], writes=[b("tmpf1")])
                        dma("sp", s_o[half], [(out_d[bb, t0 + u * 128:t0 + (u + 1) * 128, half * 512:(half + 1) * 512], tmpf[half][:])],
                            reads=[b("tmpf%d" % half)])

        P.emit(es, final_waits=list(s_o) + dbg_sems)
    return nc


def _consts():
    ident = np.eye(128, dtype=np.float32)
    s = np.arange(128)[:, None]
    q = np.arange(128)[None, :]
    maskneg = np.where(s > q, NEG, 0.0).astype(np.float32)
    slopes = 2.0 ** (-8.0 * np.arange(1, 9) / 8.0)
    biasA = np.zeros((4, 128, 512), np.float32)
    for kv in range(2):
        for kind in range(2):
            for g in range(4):
                sl = slopes[kv * 4 + g]
                if kind == 1:
                    dist = q - s
                    valid = dist >= 0
                else:
                    dist = q - s + 128
                    valid = dist < 128
                biasA[kv * 2 + kind, :, g * 128:(g + 1) * 128] = np.where(valid, -sl * dist, NEG)
    sel8 = np.zeros((128, 8 * 128), np.float32)
    for h in range(8):
        sel8[h, h * 128:(h + 1) * 128] = 1.0
    sel64 = np.zeros((128, 64), np.float32)
    sel64[64, :] = 1.0
    shiftI = np.zeros((64, 128), np.float32)
    shiftI[np.arange(64), 64 + np.arange(64)] = 1.0
    return dict(ident=ident, maskneg=maskneg, biasA=biasA, sel8=sel8, sel64=sel64, shiftI=shiftI)


_NC_CACHE = {}


def _run(inputs, ncores, NSEQ, SEQ):
    key = (NSEQ, SEQ)
    if key not in _NC_CACHE:
        _NC_CACHE[key] = build_nc(NSEQ, SEQ)
    nc = _NC_CACHE[key]
    f = lambda a: np.ascontiguousarray(np.asarray(a, dtype=np.float32))
    x = f(inputs["x"]); c = f(inputs["c"])
    pvec = np.concatenate([f(inputs["b_ada"]).reshape(72, 128), f(inputs["ln1_g"]).reshape(8, 128),
                           f(inputs["ln1_b"]).reshape(8, 128), f(inputs["ln2_g"]).reshape(8, 128),
                           f(inputs["ln2_b"]).reshape(8, 128), f(inputs["ln3_g"]).reshape(8, 128),
                           f(inputs["ln3_b"]).reshape(8, 128), f(inputs["grp_gain"]).reshape(8, 128)], axis=0)
    shared = dict(
        w_ada=f(inputs["w_ada"])[0], pvec=np.ascontiguousarray(pvec),
        gain16=f(inputs["grp_gain"]).reshape(16, 64), b_forget=f(inputs["b_forget"]).reshape(8, 1),
        sinks=f(inputs["swa_sinks"]).reshape(1, 8),
        w1g=f(inputs["ffn1_w_gate"])[0], w1u=f(inputs["ffn1_w_up"])[0], w1d=f(inputs["ffn1_w_down"])[0],
        w2g=f(inputs["ffn2_w_gate"])[0], w2u=f(inputs["ffn2_w_up"])[0], w2d=f(inputs["ffn2_w_down"])[0],
        w_in=f(inputs["w_in"])[0], w_out=f(inputs["w_out"])[0],
        w_f128=np.ascontiguousarray(np.concatenate([f(inputs["w_in"])[0][:, 2304:2312], f(inputs["w_in"])[0][:, 2184:2304]], axis=1)))
    shared.update(_consts())
    in_maps = []
    for i in range(ncores):
        m = dict(shared)
        m["x"] = np.ascontiguousarray(x[i * NSEQ:(i + 1) * NSEQ])
        m["c16"] = np.ascontiguousarray(c[i * NSEQ:(i + 1) * NSEQ].reshape(NSEQ * 8, 128))
        in_maps.append(m)
    res = run_bass_kernel_spmd(nc, in_maps, core_ids=list(range(ncores)))
    global _LAST
    _LAST = res.results
    return np.concatenate([np.asarray(r["out"]) for r in res.results], axis=0).astype(np.float32)


def kernel(**inputs):
    return _run(inputs, 8, 2, 4096)
```

```python
import math
import numpy as np
import concourse.bass as bass
import concourse.mybir as mybir
from concourse.bass_utils import run_bass_kernel_spmd

F32 = mybir.dt.float32
BF16 = mybir.dt.bfloat16
AF = mybir.ActivationFunctionType
ALU = mybir.AluOpType

D = 1024
DFF = 2816
NJ = DFF // 128
DIN = 2312
TT = 512
ALPHA = 2.0 ** 0.25
LN_EPS = 1e-5
RMS_EPS = 1e-6
NEG = -30000.0


class Buf:
    __slots__ = ("name", "last_w", "readers")

    def __init__(self, name):
        self.name = name
        self.last_w = None
        self.readers = []


class Op:
    __slots__ = ("eng", "fn", "deps", "signal", "sem", "val", "dma_sem", "ndma", "tag")

    def __init__(self, eng, fn):
        self.eng = eng
        self.fn = fn
        self.deps = []
        self.signal = False
        self.sem = None
        self.val = 0
        self.dma_sem = None
        self.ndma = 0


import types


def _snap(fn, depth=0, memo=None):
    if memo is None:
        memo = {}
    if not isinstance(fn, types.FunctionType) or fn.__closure__ is None or depth > 3:
        return fn
    if id(fn) in memo:
        return memo[id(fn)]
    cells = []
    for c in fn.__closure__:
        try:
            v = c.cell_contents
        except ValueError:
            cells.append(c)
            continue
        if isinstance(v, types.FunctionType) and v.__name__ in ("<lambda>", "fn", "f2"):
            v = _snap(v, depth + 1, memo)
        cells.append(types.CellType(v))
    g = types.FunctionType(fn.__code__, fn.__globals__, fn.__name__, fn.__defaults__, tuple(cells))
    g.__kwdefaults__ = fn.__kwdefaults__
    memo[id(fn)] = g
    return g


class Prog:
    def __init__(self, nc):
        self.nc = nc
        self.ops = []
        self.engs = {"pe": nc.tensor, "act": nc.scalar, "dve": nc.vector, "pool": nc.gpsimd, "sp": nc.sync}

    def add(self, eng, fn, reads=(), writes=(), dma_sem=None, ndma=0):
        op = Op(eng, _snap(fn))
        op.dma_sem = dma_sem
        op.ndma = ndma
        op.tag = "R:" + ",".join(r.name for r in reads[:3]) + " W:" + ",".join(w.name for w in writes[:3])
        deps = []
        for r in reads:
            if r.last_w is not None:
                deps.append(r.last_w)
        for w in writes:
            if w.last_w is not None:
                deps.append(w.last_w)
            deps.extend(w.readers)
        seen = set()
        for d in deps:
            if id(d) in seen or d is op:
                continue
            seen.add(id(d))
            if d.dma_sem is None and d.eng == "pe" and eng == "pe":
                continue
            op.deps.append(d)
            d.signal = True
        for r in reads:
            r.readers.append(op)
        for w in writes:
            w.last_w = op
            w.readers = []
        self.ops.append(op)
        return op

    def emit(self, es, final_waits):
        nc = self.nc
        sems = {e: es.enter_context(nc.semaphore("s_" + e)) for e in self.engs}
        cnt = {e: 0 for e in self.engs}
        dcnt = {}
        waited = {e: {} for e in self.engs}
        import os
        TRACE = os.environ.get("KTRACE")
        for oi, op in enumerate(self.ops):
            E = self.engs[op.eng]
            w = waited[op.eng]
            if TRACE and TRACE in op.tag:
                print("OP", oi, op.eng, op.tag, "signal", op.signal, "deps", [(d.eng, d.tag, d.sem.num if d.sem else None, d.val) for d in op.deps])
            for d in op.deps:
                key = d.sem.num
                if w.get(key, 0) >= d.val:
                    continue
                w[key] = d.val
                E.wait_ge(d.sem, d.val)
            if op.dma_sem is not None:
                op.sem = op.dma_sem
                op.fn(E, op.dma_sem)
                dcnt[op.dma_sem.num] = dcnt.get(op.dma_sem.num, 0) + 16 * op.ndma
                op.val = dcnt[op.dma_sem.num]
            else:
                ins = op.fn(E)
                if op.signal:
                    cnt[op.eng] += 1
                    op.sem = sems[op.eng]
                    op.val = cnt[op.eng]
                    ins.then_inc(sems[op.eng], 1)
        for s in final_waits:
            nc.sync.wait_ge(s, dcnt[s.num])


def build_nc(NSEQ, SEQ, dbg=(), dbg_tile=0):
    NT = SEQ // TT
    NB = SEQ // 128
    nc = bass.Bass("TRN2", target_bir_lowering=False)
    from contextlib import ExitStack
    es = ExitStack()

    def din(name, shape):
        return nc.dram_tensor(name, list(shape), F32, kind="ExternalInput").ap()

    x_d = din("x", [NSEQ, SEQ, D])
    c16_d = din("c16", [NSEQ * 8, 128])
    wada_d = din("w_ada", [D, 9 * D])
    pvec_d = din("pvec", [128, 128])
    gain_d = din("gain16", [16, 64])
    bf_d = din("b_forget", [8, 1])
    sink_d = din("sinks", [1, 8])
    w1g_d = din("w1g", [D, DFF]); w1u_d = din("w1u", [D, DFF]); w1d_d = din("w1d", [DFF, D])
    w2g_d = din("w2g", [D, DFF]); w2u_d = din("w2u", [D, DFF]); w2d_d = din("w2d", [DFF, D])
    win_d = din("w_in", [D, DIN])
    wout_d = din("w_out", [D, D])
    wf_d = din("w_f128", [D, 128])
    ident_d = din("ident", [128, 128])
    maskneg_d = din("maskneg", [128, 128])
    biasA_d = din("biasA", [4, 128, 512])
    sel8_d = din("sel8", [128, 8 * 128])
    sel64_d = din("sel64", [128, 64])
    shift_d = din("shiftI", [64, 128])
    out_d = nc.dram_tensor("out", [NSEQ, SEQ, D], F32, kind="ExternalOutput").ap()

    P = Prog(nc)
    with es:
        def sb(name, shape, dt=F32):
            return nc.alloc_sbuf_tensor("sb_" + name, list(shape), dt)

        def dsem(name):
            return es.enter_context(nc.semaphore(name))

        KTb = sb("KTb", [128, 4, SEQ], BF16)
        Vb = sb("Vb", [128, NB, 8 * 65], BF16)
        KTa = sb("KTa", [128, 5, 128], BF16)
        Va = sb("Va", [128, 5, 2 * 65], BF16)
        ncS = sb("ncS", [128, NB, 8])
        Bcols = sb("Bcols", [128, NB, 8])
        xa = sb("xa", [128, 8, TT])
        hin = sb("hin", [128, 8, TT], BF16)
        hT = sb("hT", [128, 28, TT], BF16)
        wgu = [sb("wgu%d" % i, [128, 2, 8, 256], BF16) for i in range(2)]
        wd = [sb("wd%d" % i, [128, NJ * 128], BF16) for i in range(2)]
        PT = [sb("PT%d" % i, [128, TT], BF16) for i in range(3)]
        sg = [sb("sg%d" % i, [128, TT]) for i in range(2)]
        tmpf = [sb("tmpf%d" % i, [128, TT]) for i in range(2)]
        prebf = [sb("prebf%d" % i, [128, TT], BF16) for i in range(2)]
        sqbf = [sb("sqbf%d" % i, [128, TT], BF16) for i in range(2)]
        Uev = [sb("Uev%d" % i, [128, TT]) for i in range(2)]
        ontmp = [sb("ontmp%d" % i, [64, TT]) for i in range(2)]
        mean_sb = sb("mean_sb", [128, TT])
        rstd_sb = sb("rstd_sb", [128, TT])
        rrA = sb("rrA", [128, TT])
        rrB = sb("rrB", [128, TT])
        xin = [sb("xin%d" % i, [128, D]) for i in range(2)]
        ident = sb("ident", [128, 128])
        identb = sb("identb", [128, 128], BF16)
        onesb = sb("onesb", [128, 128], BF16)
        onesf = sb("onesf", [8, 128])
        maskneg = sb("maskneg", [128, 128], BF16)
        biasA = sb("biasA", [128, 4, 512])
        sel8 = sb("sel8", [128, 8 * 128], BF16)
        sel64 = sb("sel64", [128, 64])
        sel64b = sb("sel64b", [128, 64], BF16)
        shiftI = sb("shiftI", [64, 128], BF16)
        pv = sb("pv", [128, 128])
        gainH = sb("gainH", [64, 16])
        gin = sb("gin", [16, 64])
        c16 = sb("c16", [NSEQ * 8, 128])
        scT = sb("scT", [128, NSEQ * 8])
        modsT = sb("modsT", [128, 72 * NSEQ])
        cst = sb("cst", [128, NSEQ, 16, 8])
        bfcol = sb("bfcol", [8, 1])
        esink = sb("esink", [65, 8])
        fsp = sb("fsp", [8, TT])
        ncT = sb("ncT", [8, TT])
        AUGT = sb("AUGT", [128, TT], BF16)
        ncar = [sb("ncar%d" % i, [8, 1]) for i in range(2)]
        dg8 = sb("dg8", [8, 8])
        actdummy = sb("actdummy", [1, 4])
        cbc = sb("cbc", [128, 8])
        ps = [nc.alloc_psum_tensor("ps%d" % i, [128, 512], F32) for i in range(8)]

        B = {}

        def b(name):
            if name not in B:
                B[name] = Buf(name)
            return B[name]

        bank = [b("bank%d" % i) for i in range(8)]

        def dma(eng, sem, pairs, reads=(), writes=()):
            def fn(E, s):
                for (o, i) in pairs:
                    E.dma_start(out=o, in_=i).then_inc(s, 16)
            return P.add(eng, fn, reads=reads, writes=writes, dma_sem=sem, ndma=len(pairs))

        def comp(eng, fn, reads=(), writes=()):
            return P.add(eng, fn, reads=reads, writes=writes)

        s_c = dsem("d_const")
        dma("sp", s_c, [(ident[:], ident_d), (biasA[:], biasA_d.rearrange("k p n -> p k n")),
                        (sel64[:], sel64_d), (hT[:, 16, 0:256].bitcast(F32), pvec_d), (gin[:], gain_d), (c16[:], c16_d),
                        (bfcol[:], bf_d), (esink[64:65, :], sink_d)],
            writes=[b("ident"), b("biasA"), b("sel64"), b("hT16"), b("gin"), b("c16"), b("bfcol"), b("esink")])
        s_c2 = dsem("d_const2")
        dma("pool", s_c2, [(identb[:], ident_d), (maskneg[:], maskneg_d), (sel8[:], sel8_d), (sel64b[:], sel64_d), (shiftI[:], shift_d)],
            writes=[b("identb"), b("maskneg"), b("sel8"), b("sel64b"), b("shiftI")])
        comp("dve", lambda E: E.memset(onesb[:], 1.0), writes=[b("onesb")])
        comp("dve", lambda E: E.memset(onesf[:], 1.0), writes=[b("onesf")])
        comp("dve", lambda E: E.memset(actdummy[:], 1.0), writes=[b("actdummy")])
        comp("pool", lambda E: E.memset(AUGT[:], 0.0), writes=[b("AUGT")])
        comp("pool", lambda E: E.memset(Uev[0][:], 0.0), writes=[b("Uev0")])
        comp("pool", lambda E: E.memset(Uev[1][:], 0.0), writes=[b("Uev1")])
        comp("dve", lambda E: E.memset(Vb[:], 1.0), writes=[b("Vb_all")])
        comp("dve", lambda E: E.memset(Va[:], 1.0), writes=[b("Va_all")])
        comp("act", lambda E: E.activation(out=esink[64:65, :], in_=esink[64:65, :], func=AF.Exp),
             reads=[b("esink")], writes=[b("esink")])
        comp("act", lambda E: E.mul(bfcol[:], bfcol[:], -1.0), reads=[b("bfcol")], writes=[b("bfcol")])

        comp("pe", lambda E: E.transpose(ps[0][:, 0:128], hT[:, 16, 0:256].bitcast(F32), ident[:]),
             reads=[b("hT16"), b("ident")], writes=[bank[0]])
        comp("dve", lambda E: E.tensor_copy(pv[:], ps[0][:, 0:128]), reads=[bank[0]], writes=[b("pv")])
        comp("pe", lambda E: E.transpose(ps[1][0:64, 0:16], gin[:], ident[0:16, 0:16]),
             reads=[b("gin"), b("ident")], writes=[bank[1]])
        comp("dve", lambda E: E.tensor_copy(gainH[:], ps[1][0:64, 0:16]), reads=[bank[1]], writes=[b("gainH")])
        comp("pe", lambda E: E.transpose(ps[2][:, 0:NSEQ * 8], c16[:], ident[0:NSEQ * 8, 0:NSEQ * 8]),
             reads=[b("c16"), b("ident")], writes=[bank[2]])
        comp("act", lambda E: E.activation(out=scT[:], in_=ps[2][:, 0:NSEQ * 8], func=AF.Silu),
             reads=[bank[2]], writes=[b("scT")])

        wa_f = [hT[:, 0:8, :].bitcast(F32), hT[:, 8:16, :].bitcast(F32)]
        wa_bufs = [[b("hT%d" % j) for j in range(0, 8)], [b("hT%d" % j) for j in range(8, 16)]]
        s_wa = [dsem("d_wa0"), dsem("d_wa1")]
        wada_v = wada_d.rearrange("(c p) n -> p c n", p=128)
        NG = 36
        for g in range(NG):
            sl = g % 2
            wv = wa_f[sl]
            dma("sp", s_wa[sl], [(wv, wada_v[:, :, g * 256:(g + 1) * 256])], writes=wa_bufs[sl])
            for jj in range(2):
                m8 = g * 2 + jj

                def fn(E, wv=wv, jj=jj, m8=m8):
                    for c in range(8):
                        rhs = scT[:, c:c + 8 * (NSEQ - 1) + 1:8] if NSEQ > 1 else scT[:, c:c + 1]
                        ins = E.matmul(ps[3][:, m8 * NSEQ:(m8 + 1) * NSEQ], wv[:, c, jj * 128:(jj + 1) * 128],
                                       rhs, start=(c == 0), stop=(c == 7), skip_group_check=True)
                    return ins
                comp("pe", fn, reads=wa_bufs[sl] + [b("scT")], writes=[bank[3]])
        comp("dve", lambda E: E.tensor_copy(modsT[:], ps[3][:, 0:72 * NSEQ]), reads=[bank[3]], writes=[b("modsT")])

        mv = modsT[:].rearrange("p (m c b) -> p b m c", m=9, c=8, b=NSEQ)

        def pvv(r):
            return pv[:, r * 8:(r + 1) * 8]
        BADA = lambda m: pvv(m)
        LNG = lambda i: pvv(9 + 2 * (i - 1))
        LNB = lambda i: pvv(10 + 2 * (i - 1))

        def cop(f, extra=()):
            comp("dve", f, reads=[b("cst"), b("modsT"), b("pv")] + list(extra), writes=[b("cst")])

        for bb in range(NSEQ):
            C = lambda k, bb=bb: cst[:, bb, k, :]
            M = lambda m, bb=bb: mv[:, bb, m, :]
            for (m, k) in [(0, 1), (1, 0), (2, 2), (3, 6), (4, 5), (5, 7), (6, 11), (7, 10), (8, 12)]:
                cop(lambda E, m=m, k=k: E.tensor_tensor(out=C(k), in0=M(m), in1=BADA(m), op=ALU.add))
            for k in (0, 5, 10):
                cop(lambda E, k=k: E.tensor_scalar_add(C(k), C(k), 1.0))
            cop(lambda E: E.tensor_scalar(out=C(2), in0=C(2), scalar1=1.0, scalar2=0.5, op0=ALU.add, op1=ALU.mult))
            cop(lambda E: E.tensor_scalar_add(C(7), C(7), 1.0))
            cop(lambda E: E.tensor_scalar(out=C(12), in0=C(12), scalar1=1.0, scalar2=0.5, op0=ALU.add, op1=ALU.mult))
            cop(lambda E: E.tensor_tensor(out=C(15), in0=LNB(1), in1=C(5), op=ALU.mult))
            cop(lambda E: E.tensor_tensor(out=C(6), in0=C(6), in1=C(15), op=ALU.add))
            cop(lambda E: E.tensor_tensor(out=C(5), in0=C(5), in1=LNG(1), op=ALU.mult))
            cop(lambda E: E.tensor_tensor(out=C(15), in0=LNB(2), in1=C(10), op=ALU.mult))
            cop(lambda E: E.tensor_tensor(out=C(11), in0=C(11), in1=C(15), op=ALU.add))
            cop(lambda E: E.tensor_tensor(out=C(10), in0=C(10), in1=LNG(2), op=ALU.mult))
            cop(lambda E: E.tensor_scalar_mul(C(0), C(0), 1.0 / ALPHA))
            cop(lambda E: E.tensor_scalar_mul(C(3), LNG(1), ALPHA))
            cop(lambda E: E.tensor_scalar_mul(C(4), LNB(1), ALPHA))
            cop(lambda E: E.tensor_scalar_mul(C(8), LNG(2), ALPHA))
            cop(lambda E: E.tensor_scalar_mul(C(9), LNB(2), ALPHA))
            cop(lambda E: E.tensor_copy(C(13), LNG(3)))
            cop(lambda E: E.tensor_copy(C(14), LNB(3)))

        dbg_sems = []

        def dump(tag, ap_fn, bufs, dt=F32):
            if tag not in dbg:
                return
            ap = ap_fn()
            dd = nc.dram_tensor("dbg_" + tag, list(ap.shape), dt, kind="ExternalOutput").ap()
            sm = dsem("d_dbg_" + tag)
            dbg_sems.append(sm)
            dma("sp", sm, [(dd, ap)], reads=bufs)

        dump("cst", lambda: cst[:], [b("cst")])
        dump("modsT", lambda: modsT[:], [b("modsT")])
        dump("pv", lambda: pv[:], [b("pv")])

        def CS(bb, k, c):
            return cst[:, bb, k, c:c + 1]

        s_wgu = [dsem("d_wgu0"), dsem("d_wgu1")]
        s_wd = [dsem("d_wd0"), dsem("d_wd1")]
        s_wgu_sw = [dsem("d_wgusw0"), dsem("d_wgusw1")]
        s_wd_sw = [dsem("d_wdsw0"), dsem("d_wdsw1")]
        wctr = {"gu": 0, "d": 0}

        scr = {}
        s_st = {"gu0": dsem("d_stg0"), "gu1": dsem("d_stg1"), "d0": dsem("d_std0"), "d1": dsem("d_std1")}

        def load_gu(key, pieces_g, pieces_u):
            sl = wctr["gu"] % 2
            wctr["gu"] += 1
            img = wgu[sl][:].rearrange("p a c n -> p (a c n)")
            if key in scr:
                dma("sp", s_wgu[sl], [(img, scr[key])], reads=[b("scr_" + key)], writes=[b("wgu%d" % sl)])
                return sl
            pairs = []
            for half, pcs in ((0, pieces_g), (1, pieces_u)):
                for (src, off, n) in pcs:
                    pairs.append((wgu[sl][:, half, :, off:off + n], src))
            dma("pool", s_wgu_sw[sl], pairs, writes=[b("wgu%d" % sl)])
            scr[key] = nc.dram_tensor("scr_" + key, [128, 2 * 8 * 256], BF16, kind="Internal").ap()
            dma("sp", s_st["gu%d" % sl], [(scr[key], img)], reads=[b("wgu%d" % sl)], writes=[b("scr_" + key)])
            return sl

        def load_d(key, pairs_fn, np_=128, nf=NJ * 128):
            sl = wctr["d"] % 2
            wctr["d"] += 1
            img = wd[sl][0:np_, 0:nf]
            if key in scr:
                dma("sp", s_wd[sl], [(img, scr[key])], reads=[b("scr_" + key)], writes=[b("wd%d" % sl)])
                return sl
            dma("pool", s_wd_sw[sl], pairs_fn(wd[sl]), writes=[b("wd%d" % sl)])
            scr[key] = nc.dram_tensor("scr_" + key, [np_, nf], BF16, kind="Internal").ap()
            dma("sp", s_st["d%d" % sl], [(scr[key], img)], reads=[b("wd%d" % sl)], writes=[b("scr_" + key)])
            return sl

        st = {"pb": 0, "tn": 0}

        def residual_chunk(bb, n, ybanks, scal, gate_k, first, rr=None):
            G = CS(bb, gate_k, n)
            if rr is None:
                (yb,) = ybanks
                comp("dve", lambda E: E.scalar_tensor_tensor(out=xa[:, n, :], in0=ps[yb][:, :], scalar=G, in1=xa[:, n, :],
                                                             op0=ALU.mult, op1=ALU.add),
                     reads=[bank[yb], b("cst"), b("xa%d" % n)], writes=[b("xa%d" % n)])
            else:
                ya, yb2 = ybanks
                t = tmpf[st["tn"] % 2]; tb = b("tmpf%d" % (st["tn"] % 2)); st["tn"] += 1
                comp("dve", lambda E: E.tensor_tensor(out=t[:], in0=ps[ya][:, :], in1=rrA[:], op=ALU.mult),
                     reads=[bank[ya], b("rrA")], writes=[tb])
                t2 = tmpf[st["tn"] % 2]; tb2 = b("tmpf%d" % (st["tn"] % 2)); st["tn"] += 1
                comp("dve", lambda E: E.tensor_tensor(out=t2[:], in0=ps[yb2][:, :], in1=rrB[:], op=ALU.mult),
                     reads=[bank[yb2], b("rrB")], writes=[tb2])
                comp("dve", lambda E: E.tensor_tensor(out=t[:], in0=t[:], in1=t2[:], op=ALU.add),
                     reads=[tb, tb2], writes=[tb])
                comp("dve", lambda E: E.scalar_tensor_tensor(out=xa[:, n, :], in0=t[:], scalar=G, in1=xa[:, n, :],
                                                             op0=ALU.mult, op1=ALU.add),
                     reads=[tb, b("cst"), b("xa%d" % n)], writes=[b("xa%d" % n)])
            k = st["pb"] % 2; st["pb"] += 1
            comp("act", lambda E: E.activation(out=sqbf[k][:], in_=xa[:, n, :], func=AF.Square),
                 reads=[b("xa%d" % n)], writes=[b("sqbf%d" % k)])
            comp("dve", lambda E: E.tensor_copy(prebf[k][:], xa[:, n, :]),
                 reads=[b("xa%d" % n)], writes=[b("prebf%d" % k)])

            def fn(E):
                E.matmul(ps[6][:, :], onesb[:], prebf[k][:], start=first, stop=False, skip_group_check=True)
                return E.matmul(ps[7][:, :], onesb[:], sqbf[k][:], start=first, stop=False, skip_group_check=True)
            return ("pe", fn, [b("onesb"), b("prebf%d" % k), b("sqbf%d" % k)], [bank[6], bank[7]])

        def layernorm(bb, Rs_k, Rb_k, Hs_k, Hb_k, final=False):
            inv = 1.0 / D
            comp("dve", lambda E: E.tensor_scalar_mul(mean_sb[:], ps[6][:, :], inv), reads=[bank[6]], writes=[b("mean")])
            comp("dve", lambda E: E.tensor_tensor(out=rstd_sb[:], in0=mean_sb[:], in1=mean_sb[:], op=ALU.mult),
                 reads=[b("mean")], writes=[b("rstd")])
            comp("dve", lambda E: E.scalar_tensor_tensor(out=rstd_sb[:], in0=ps[7][:, :], scalar=inv, in1=rstd_sb[:],
                                                         op0=ALU.mult, op1=ALU.subtract),
                 reads=[bank[7], b("rstd")], writes=[b("rstd")])
            comp("act", lambda E: E.activation(out=rstd_sb[:], in_=rstd_sb[:], func=AF.Ln, bias=LN_EPS),
                 reads=[b("rstd")], writes=[b("rstd")])
            comp("act", lambda E: E.activation(out=rstd_sb[:], in_=rstd_sb[:], func=AF.Exp, scale=-0.5),
                 reads=[b("rstd")], writes=[b("rstd")])
            for c in range(8):
                t = tmpf[st["tn"] % 2]; tb = b("tmpf%d" % (st["tn"] % 2)); st["tn"] += 1
                comp("dve", lambda E, c=c, t=t: E.tensor_tensor(out=t[:], in0=xa[:, c, :], in1=mean_sb[:], op=ALU.subtract),
                     reads=[b("xa%d" % c), b("mean")], writes=[tb])
                comp("dve", lambda E, t=t: E.tensor_tensor(out=t[:], in0=t[:], in1=rstd_sb[:], op=ALU.mult),
                     reads=[tb, b("rstd")], writes=[tb])
                if not final:
                    comp("act", lambda E, c=c, t=t: E.activation(out=hin[:, c, :], in_=t[:], func=AF.Identity,
                                                                 scale=CS(bb, Hs_k, c), bias=CS(bb, Hb_k, c)),
                         reads=[tb, b("cst")], writes=[b("hin%d" % c)])
                comp("act", lambda E, c=c, t=t: E.activation(out=xa[:, c, :], in_=t[:], func=AF.Identity,
                                                             scale=CS(bb, Rs_k, c), bias=CS(bb, Rb_k, c)),
                     reads=[tb, b("cst")], writes=[b("xa%d" % c)])

        def ffn(fid, bb, wg_d, wu_d, wdn_d, gate_k, Rs_k, Rb_k, Hs_k, Hb_k, final):
            wg_v = wg_d.rearrange("(c p) n -> p c n", p=128)
            wu_v = wu_d.rearrange("(c p) n -> p c n", p=128)
            wdn_v = wdn_d.rearrange("(j p) n -> p j n", p=128)
            hinb = [b("hin%d" % c) for c in range(8)]
            for slab in range(NJ // 2):
                c0 = slab * 256
                sl = load_gu("f%dgu%d" % (fid, slab), [(wg_v[:, :, c0:c0 + 256], 0, 256)], [(wu_v[:, :, c0:c0 + 256], 0, 256)])
                for jj in range(2):
                    j = slab * 2 + jj
                    gb = j % 2
                    ub = 2 + j % 2

                    def fn(E, sl=sl, jj=jj, gb=gb, ub=ub):
                        for c in range(8):
                            E.matmul(ps[gb][:, :], wgu[sl][:, 0, c, jj * 128:(jj + 1) * 128], hin[:, c, :],
                                     start=(c == 0), stop=(c == 7))
                        for c in range(8):
                            ins = E.matmul(ps[ub][:, :], wgu[sl][:, 1, c, jj * 128:(jj + 1) * 128], hin[:, c, :],
                                           start=(c == 0), stop=(c == 7))
                        return ins
                    comp("pe", fn, reads=[b("wgu%d" % sl)] + hinb, writes=[bank[gb], bank[ub]])
                    k = j % 2
                    comp("act", lambda E, gb=gb, k=k: E.activation(out=sg[k][:], in_=ps[gb][:, :], func=AF.Silu),
                         reads=[bank[gb]], writes=[b("sg%d" % k)])
                    comp("dve", lambda E, ub=ub, k=k, j=j: E.tensor_tensor(out=hT[:, j, :], in0=sg[k][:], in1=ps[ub][:, :],
                                                                          op=ALU.mult),
                         reads=[bank[ub], b("sg%d" % k)], writes=[b("hT%d" % j)])
            hTb = [b("hT%d" % j) for j in range(NJ)]
            pend = None
            for n in range(8):
                sl = load_d("f%dd%d" % (fid, n), lambda w, n=n: [
                    (w.rearrange("p (j n) -> p j n", j=NJ)[:, 0:11, :], wdn_v[:, 0:11, n * 128:(n + 1) * 128]),
                    (w.rearrange("p (j n) -> p j n", j=NJ)[:, 11:22, :], wdn_v[:, 11:22, n * 128:(n + 1) * 128])])
                yb = 4 + n % 2

                def fn(E, sl=sl, yb=yb):
                    wv = wd[sl].rearrange("p (j n) -> p j n", j=NJ)
                    for j in range(NJ):
                        ins = E.matmul(ps[yb][:, :], wv[:, j, :], hT[:, j, :], start=(j == 0), stop=(j == NJ - 1))
                    return ins
                comp("pe", fn, reads=[b("wd%d" % sl)] + hTb, writes=[bank[yb]])
                if pend is not None:
                    comp(pend[0], pend[1], reads=pend[2], writes=pend[3])
                pend = residual_chunk(bb, n, (yb,), None, gate_k, first=(n == 0))
            comp(pend[0], pend[1], reads=pend[2], writes=pend[3])
            layernorm(bb, Rs_k, Rb_k, Hs_k, Hb_k, final=final)

        s_x = [dsem("d_x0"), dsem("d_x1")]
        s_o = [dsem("d_o0"), dsem("d_o1")]
        xctr = {"n": 0}
        preloaded = set()
        win_v = win_d.rearrange("(c p) n -> p c n", p=128)
        wout_v = wout_d.rearrange("(h p) n -> p h n", p=64)
        hinb = [b("hin%d" % c) for c in range(8)]

        for bb in range(NSEQ):
            for it in range(NT):
                t0 = it * TT
                for u in range(4):
                    k = u % 2
                    if (bb, it, u) not in preloaded:
                        dma("sp", s_x[k], [(xin[k][:], x_d[bb, t0 + u * 128:t0 + (u + 1) * 128, :])],
                            writes=[b("xin%d" % k)])
                    for half in range(2):
                        pb = half

                        def fn(E, k=k, half=half, pb=pb):
                            for cc in range(4):
                                c = half * 4 + cc
                                ins = E.transpose(ps[pb][:, cc * 128:(cc + 1) * 128], xin[k][:, c * 128:(c + 1) * 128], ident[:])
                            return ins
                        comp("pe", fn, reads=[b("xin%d" % k), b("ident")], writes=[bank[pb]])
                        eng = "act" if half == 0 else "dve"
                        if eng == "act":
                            f2 = lambda E, half=half, pb=pb, u=u: E.mul(
                                xa[:, half * 4:half * 4 + 4, u * 128:(u + 1) * 128],
                                ps[pb][:, :].rearrange("p (c n) -> p c n", c=4), ALPHA)
                        else:
                            f2 = lambda E, half=half, pb=pb, u=u: E.tensor_scalar_mul(
                                xa[:, half * 4:half * 4 + 4, u * 128:(u + 1) * 128],
                                ps[pb][:, :].rearrange("p (c n) -> p c n", c=4), ALPHA)
                        comp(eng, f2, reads=[bank[pb]], writes=[b("xa%d" % c) for c in range(half * 4, half * 4 + 4)])
                for c in range(8):
                    comp("act", lambda E, c=c: E.activation(out=hin[:, c, :], in_=xa[:, c, :], func=AF.Identity,
                                                            scale=CS(bb, 0, c), bias=CS(bb, 1, c)),
                         reads=[b("xa%d" % c), b("cst")], writes=[b("hin%d" % c)])

                DB = (bb == 0 and it == dbg_tile)
                xab = [b("xa%d" % c) for c in range(8)]
                if DB:
                    dump("xa0", lambda: xa[:], xab)
                    dump("hin0", lambda: hin[:], hinb, BF16)
                ffn(1, bb, w1g_d, w1u_d, w1d_d, 2, 3, 4, 5, 6, final=False)
                if DB:
                    dump("hT1", lambda: hT[:, 0:22, :], [b("hT%d" % j) for j in range(22)], BF16)
                    dump("xa1", lambda: xa[:], xab)
                    dump("hin1", lambda: hin[:], hinb, BF16)

                QTa = lambda cp: hT[:, 16 + cp, :]
                QTb = lambda cp: hT[:, 20 + cp, :]
                def proj_fm(sl, half, coff, dst_fn, dst_bufs, scale, pbk, eng="act", M=128, qpair=None):
                    def fn(E):
                        for c in range(8):
                            ins = E.matmul(ps[pbk][0:M, :], wgu[sl][:, half, c, coff:coff + M], hin[:, c, :],
                                           start=(c == 0), stop=(c == 7))
                        return ins
                    comp("pe", fn, reads=[b("wgu%d" % sl)] + hinb, writes=[bank[pbk]])
                    if qpair is not None:
                        h0 = 2 * qpair
                        comp("act", lambda E: E.mul(hT[0:64, 20 + h0, :], ps[pbk][0:64, :], scale),
                             reads=[bank[pbk]], writes=[b("hT%d" % (20 + h0))])
                        comp("dve", lambda E: E.tensor_scalar_mul(hT[64:128, 21 + h0, :], ps[pbk][64:128, :], scale),
                             reads=[bank[pbk]], writes=[b("hT%d" % (21 + h0))])
                        comp("pool", lambda E: E.memset(hT[64:128, 20 + h0, :], 0.0), writes=[b("hT%d" % (20 + h0))])
                        comp("pool", lambda E: E.memset(hT[0:64, 21 + h0, :], 0.0), writes=[b("hT%d" % (21 + h0))])
                        return
                    if eng == "act":
                        comp("act", lambda E: E.mul(dst_fn(), ps[pbk][0:M, :], scale), reads=[bank[pbk]], writes=dst_bufs)
                    else:
                        comp("dve", lambda E: E.tensor_scalar_mul(dst_fn(), ps[pbk][0:M, :], scale), reads=[bank[pbk]], writes=dst_bufs)

                pg = []
                for cp in range(2):
                    pg += [(win_v[:, :, cp * 64:(cp + 1) * 64], cp * 128, 64),
                           (win_v[:, :, (cp + 4) * 64:(cp + 5) * 64], cp * 128 + 64, 64)]
                pu = []
                for cp in range(2, 4):
                    pu += [(win_v[:, :, cp * 64:(cp + 1) * 64], (cp - 2) * 128, 64),
                           (win_v[:, :, (cp + 4) * 64:(cp + 5) * 64], (cp - 2) * 128 + 64, 64)]
                sl = load_gu("winA", pg, pu)
                for cp in range(4):
                    proj_fm(sl, cp // 2, (cp % 2) * 128, lambda cp=cp: QTa(cp), [b("hT%d" % (16 + cp))], 0.125,
                            cp % 4, eng=("act" if cp % 2 == 0 else "dve"))
                sl = load_gu("winB", [(win_v[:, :, 512:768], 0, 256)], [(win_v[:, :, 768:1024], 0, 256)])
                proj_fm(sl, 0, 0, lambda: KTa[:, 1:5, :].rearrange("p a n -> p (a n)"),
                        [b("KTa%d" % a) for a in range(1, 5)], 1.0, 0, eng="dve")
                for u in range(4):
                    pbk = 4 + u % 2

                    def fn(E, sl=sl, u=u, pbk=pbk):
                        for c in range(8):
                            ins = E.matmul(ps[pbk][:, 0:128], hin[:, c, u * 128:(u + 1) * 128], wgu[sl][:, 0, c, 128:256],
                                           start=(c == 0), stop=(c == 7))
                        return ins
                    comp("pe", fn, reads=[b("wgu%d" % sl)] + hinb, writes=[bank[pbk]])
                    comp("act", lambda E, u=u, pbk=pbk: E.copy(
                        Va[:, 1 + u, :].rearrange("p (k e) -> p k e", k=2)[:, :, 0:64],
                        ps[pbk][:, 0:128].rearrange("p (k e) -> p k e", k=2)),
                        reads=[bank[pbk], b("Va_all")], writes=[b("Va%d" % (1 + u))])
                for cp in range(2):
                    proj_fm(sl, 1, cp * 128, None, None, 0.125, 1 + cp, qpair=cp)
                sl = load_gu("winC", [(win_v[:, :, 1024:1280], 0, 256)], [(win_v[:, :, 1280:1536], 0, 256)])
                for cp in range(2, 4):
                    proj_fm(sl, 0, (cp - 2) * 128, None, None, 0.125, cp, qpair=cp)
                for cp in range(2):
                    proj_fm(sl, 1, cp * 128, lambda cp=cp: KTb[:, cp, t0:t0 + TT], [b("KTb%d" % cp)], 1.0, cp,
                            eng=("act" if cp == 0 else "dve"))
                sl = load_gu("winD", [(win_v[:, :, 1536:1792], 0, 256)], [(win_v[:, :, 1792:2048], 0, 256)])
                for cp in range(2, 4):
                    proj_fm(sl, 0, (cp - 2) * 128, lambda cp=cp: KTb[:, cp, t0:t0 + TT], [b("KTb%d" % cp)], 1.0, cp,
                            eng=("act" if cp == 2 else "dve"))
                slD = sl
                sl = load_gu("winE", [(win_v[:, :, 2048:2304], 0, 256)], [(wf_d.rearrange("(c p) n -> p c n", p=128), 0, 128)])
                slE = sl
                for u in range(4):
                    pbk = 4 + u % 2

                    def fn(E, u=u, pbk=pbk):
                        for c in range(8):
                            E.matmul(ps[pbk][:, 0:256], hin[:, c, u * 128:(u + 1) * 128], wgu[slD][:, 1, c, 0:256],
                                     start=(c == 0), stop=(c == 7))
                        for c in range(8):
                            ins = E.matmul(ps[pbk][:, 256:512], hin[:, c, u * 128:(u + 1) * 128], wgu[slE][:, 0, c, 0:256],
                                           start=(c == 0), stop=(c == 7), skip_group_check=True)
                        return ins
                    comp("pe", fn, reads=[b("wgu%d" % slD), b("wgu%d" % slE)] + hinb, writes=[bank[pbk]])
                    blk = it * 4 + u
                    comp("act" if u % 2 == 0 else "dve",
                         (lambda E, blk=blk, pbk=pbk: E.copy(
                             Vb[:, blk, :].rearrange("p (h e) -> p h e", h=8)[:, :, 0:64],
                             ps[pbk][:, :].rearrange("p (h e) -> p h e", h=8))) if u % 2 == 0 else
                         (lambda E, blk=blk, pbk=pbk: E.tensor_copy(
                             Vb[:, blk, :].rearrange("p (h e) -> p h e", h=8)[:, :, 0:64],
                             ps[pbk][:, :].rearrange("p (h e) -> p h e", h=8))),
                         reads=[bank[pbk], b("Vb_all")], writes=[b("Vb%d" % blk)])
                def fn(E):
                    for c in range(8):
                        ins = E.matmul(ps[6][:, :], wgu[slE][:, 1, c, 0:128], hin[:, c, :], start=(c == 0), stop=(c == 7))
                    return ins
                comp("pe", fn, reads=[b("wgu%d" % slE)] + hinb, writes=[bank[6]])
                if "rawf" in dbg:
                    comp("act", lambda E: E.copy(fsp[:], ps[6][0:8, :]), reads=[bank[6], b("bfcol")], writes=[b("fsp")])
                else:
                    comp("act", lambda E: E.activation(out=fsp[:], in_=ps[6][0:8, :], func=AF.Exp, scale=-1.0, bias=bfcol[:]),
                         reads=[bank[6], b("bfcol")], writes=[b("fsp")])
                    comp("act", lambda E: E.activation(out=fsp[:], in_=fsp[:], func=AF.Ln, bias=1.0),
                         reads=[b("fsp")], writes=[b("fsp")])
                cur = ncar[it % 2]; prv = ncar[(it + 1) % 2]
                curb = b("ncar%d" % (it % 2)); prvb = b("ncar%d" % ((it + 1) % 2))
                if it == 0:
                    comp("dve", lambda E: E.memset(prv[:], 0.0), writes=[prvb])
                comp("dve", lambda E: E.tensor_tensor_scan(out=ncT[:], data0=fsp[:], data1=fsp[:], initial=prv[:],
                                                           op0=ALU.add, op1=ALU.max),
                     reads=[b("fsp"), prvb], writes=[b("ncT")])
                comp("dve", lambda E: E.tensor_copy(cur[:], ncT[:, TT - 1:TT]), reads=[b("ncT")], writes=[curb])
                comp("dve", lambda E: E.tensor_scalar(out=AUGT[0:8, :], in0=ncT[:], scalar1=prv[:], scalar2=-1.0,
                                                      op0=ALU.subtract, op1=ALU.mult),
                     reads=[b("ncT"), prvb], writes=[b("AUGT")])
                def fn(E):
                    for u in range(4):
                        ins = E.matmul(ps[7][:, u * 8:(u + 1) * 8], ncT[:, u * 128:(u + 1) * 128], ident[0:8, 0:8],
                                       start=True, stop=True, skip_group_check=True)
                    return ins
                comp("pe", fn, reads=[b("ncT"), b("ident")], writes=[bank[7]])
                comp("dve", lambda E: E.tensor_copy(ncS[:, it * 4:it * 4 + 4, :],
                                                    ps[7][:, 0:32].rearrange("p (u h) -> p u h", u=4)),
                     reads=[bank[7]], writes=[b("ncS")])
                comp("dve", lambda E: E.tensor_scalar_mul(dg8[:], ident[0:8, 0:8], prv[:]),
                     reads=[b("ident"), prvb], writes=[b("dg8")])
                comp("pe", lambda E: E.matmul(ps[6][:, 0:8], onesf[:], dg8[:], start=True, stop=True),
                     reads=[b("onesf"), b("dg8")], writes=[bank[6]])
                comp("dve", lambda E: E.tensor_copy(cbc[:], ps[6][:, 0:8]), reads=[bank[6]], writes=[b("cbc")])
                nblk = it * 4 + 4
                comp("dve", lambda E: E.tensor_tensor(out=Bcols[:, 0:nblk, :], in0=ncS[:, 0:nblk, :],
                                                      in1=cbc[:].unsqueeze(1).to_broadcast([128, nblk, 8]), op=ALU.subtract),
                     reads=[b("ncS"), b("cbc")], writes=[b("Bcols")])

                hctr = {"u": 0, "p": 0, "t": 0, "o": 0, "hl": 0}

                def finish_steps(obank, sink_kv, ssq_fn, gain_fn):
                    k = hctr["u"] % 2; hctr["u"] += 1
                    U = Uev[k]; Ub = b("Uev%d" % k)
                    comp("act", lambda E: E.copy(U[0:64, :], ps[obank][0:64, :]), reads=[bank[obank]], writes=[Ub])
                    if sink_kv is not None:
                        kv = sink_kv
                        comp("dve", lambda E: E.tensor_tensor(
                            out=U[64:65, :].rearrange("p (g q) -> p g q", g=4),
                            in0=ps[obank][64:65, :].rearrange("p (g q) -> p g q", g=4),
                            in1=esink[64:65, kv * 4:kv * 4 + 4].unsqueeze(2).to_broadcast([1, 4, 128]), op=ALU.add),
                            reads=[bank[obank], b("esink")], writes=[b("Urow%d" % k)])
                        comp("act", lambda E: E.activation(out=U[64:65, :], in_=U[64:65, :], func=AF.Ln),
                             reads=[b("Urow%d" % k)], writes=[b("Urow%d" % k)])
                    else:
                        comp("act", lambda E: E.activation(out=U[64:65, :], in_=ps[obank][64:65, :], func=AF.Ln),
                             reads=[bank[obank]], writes=[b("Urow%d" % k)])
                    comp("act", lambda E: E.activation(out=U[64:65, :], in_=U[64:65, :], func=AF.Exp, scale=-1.0),
                         reads=[b("Urow%d" % k)], writes=[b("Urow%d" % k)])
                    hl = hctr["hl"] % 4; hctr["hl"] += 1
                    hiT = hin[:, 2 * hl, :]; loT = hin[:, 2 * hl + 1, :]
                    hib = b("hin%d" % (2 * hl)); lob = b("hin%d" % (2 * hl + 1))
                    comp("dve", lambda E: E.tensor_copy(hiT[64:65, :], U[64:65, :]),
                         reads=[b("Urow%d" % k)], writes=[hib])
                    comp("dve", lambda E: E.tensor_tensor(out=loT[64:65, :], in0=U[64:65, :], in1=hiT[64:65, :],
                                                          op=ALU.subtract),
                         reads=[b("Urow%d" % k), hib], writes=[lob])
                    rb = obank
                    hctr["p"] += 1
                    k2 = hctr["o"] % 2; hctr["o"] += 1
                    on = ontmp[k2]; onb = b("ontmp%d" % k2)
                    kq = st["pb"] % 2; st["pb"] += 1

                    def step1():
                        def fnb(E):
                            E.matmul(ps[rb][0:64, :], sel64b[:], hiT, start=True, stop=False)
                            return E.matmul(ps[rb][0:64, :], sel64b[:], loT, start=False, stop=True)
                        comp("pe", fnb, reads=[b("sel64b"), hib, lob], writes=[bank[rb]])
                        comp("dve", lambda E: E.tensor_tensor(out=on[:], in0=U[0:64, :], in1=ps[rb][0:64, :], op=ALU.mult),
                             reads=[Ub, bank[rb]], writes=[onb])
                        comp("act", lambda E: E.activation(out=sqbf[kq][0:64, :], in_=on[:], func=AF.Square),
                             reads=[onb], writes=[b("sqbf%d" % kq)])
                        gain_fn(on, onb)

                    def step2():
                        ssq_fn(kq)
                    return [step1, step2]

                sblocks = []
                for u in range(4):
                    nblk_q = it * 4 + u
                    for kv in range(2):
                        kbs = ([(u, 0)] if nblk_q > 0 else []) + [(u + 1, 1)]
                        for ik, (slot, kind) in enumerate(kbs):
                            sblocks.append(("s", u, kv, ik, len(kbs), slot, kind))
                nkb = it * 4 + 4
                fblocks = [("f", h, kb) for h in range(8) for kb in range(nkb)]
                nF, nS = len(fblocks), len(sblocks)
                blocks = []
                si = 0
                for fi, fb in enumerate(fblocks):
                    while si < nS and si * (nF - 8) <= fi * nS:
                        blocks.append(sblocks[si]); si += 1
                    blocks.append(fb)
                while si < nS:
                    blocks.append(sblocks[si]); si += 1
                SB = [0, 1, 4]
                LOOK = 2
                hctr["fox"] = True
                deferred = []
                fin = {"g": 0}

                def flush_upto(g):
                    while deferred and deferred[0][2] <= g:
                        deferred.pop(0)[1]()

                def flush_bank(ob):
                    last = -1
                    for k_, e in enumerate(deferred):
                        if e[3] == ob:
                            last = k_
                    for _ in range(last + 1):
                        deferred.pop(0)[1]()

                def emit_qk(i):
                    blk = blocks[i]
                    sbk = SB[i % 3]
                    pk = i % 3
                    if blk[0] == "f":
                        _, h, kb = blk
                        cp = h // 2
                        r = kb - it * 4
                        c0 = 0 if r < 0 else r * 128

                        def fn(E):
                            ins = E.matmul(ps[sbk][:, c0:TT], KTb[:, cp, kb * 128:(kb + 1) * 128],
                                           hT[:, 20 + h, c0:TT], start=True, stop=False, skip_group_check=True)
                            ins = E.matmul(ps[sbk][:, c0:TT], sel8[:, h * 128:(h + 1) * 128], AUGT[:, c0:TT],
                                           start=False, stop=(r < 0), skip_group_check=True)
                            if r >= 0:
                                ins = E.matmul(ps[sbk][:, c0:c0 + 128], identb[:], maskneg[:], start=False, stop=True,
                                               skip_group_check=True)
                            return ins
                        comp("pe", fn, reads=[b("KTb%d" % cp), b("hT%d" % (20 + h)), b("sel8"), b("AUGT"),
                                              b("identb"), b("maskneg")], writes=[bank[sbk]])
                        comp("act", lambda E: E.activation(
                            out=PT[pk][:, c0:TT], in_=ps[sbk][:, c0:TT], func=AF.Exp, bias=Bcols[:, kb, h:h + 1]),
                            reads=[bank[sbk], b("Bcols")], writes=[b("PT%d" % pk)])
                    else:
                        _, u, kv, ik, nk, slot, kind = blk
                        comp("pe", lambda E: E.matmul(
                            ps[sbk][:, :].rearrange("p (g q) -> p g q", g=4),
                            KTa[kv * 64:(kv + 1) * 64, slot, :],
                            hT[kv * 64:(kv + 1) * 64, 16:20, u * 128:(u + 1) * 128], start=True, stop=True),
                            reads=[b("KTa%d" % slot)] + [b("hT%d" % (16 + g)) for g in range(4)], writes=[bank[sbk]])
                        tk = st["tn"] % 2; st["tn"] += 1
                        t = tmpf[tk]; tb = b("tmpf%d" % tk)
                        comp("dve", lambda E: E.tensor_tensor(
                            out=t[:], in0=ps[sbk][:, :], in1=biasA[:, kv * 2 + kind, :], op=ALU.add),
                            reads=[bank[sbk], b("biasA")], writes=[tb])
                        comp("act", lambda E: E.activation(out=PT[pk][:], in_=t[:], func=AF.Exp),
                             reads=[tb], writes=[b("PT%d" % pk)])

                def emit_pv(i):
                    blk = blocks[i]
                    pk = i % 3
                    if blk[0] == "f":
                        _, h, kb = blk
                        r = kb - it * 4
                        c0 = 0 if r < 0 else r * 128
                        ob = 2 + h % 2
                        if kb == 0:
                            flush_bank(ob)
                        comp("pe", lambda E: E.matmul(
                            ps[ob][0:65, c0:TT], Vb[:, kb, h * 65:(h + 1) * 65], PT[pk][:, c0:TT],
                            start=(kb == 0), stop=(kb == nkb - 1), skip_group_check=True),
                            reads=[b("Vb%d" % kb), b("Vb_all"), b("PT%d" % pk)], writes=[bank[ob]])
                        if kb == nkb - 1:
                            def ssq_fn(kq, h=h):
                                comp("pe", lambda E: E.matmul(ps[7][:, :], onesb[0:64, :], sqbf[kq][0:64, :],
                                                              start=(h == 0), stop=(h == 7), skip_group_check=True),
                                     reads=[b("onesb"), b("sqbf%d" % kq)], writes=[bank[7]])

                            def gain_fn(on, onb, h=h):
                                comp("dve", lambda E: E.tensor_scalar_mul(hT[0:64, 8 + h, :], on[:], gainH[:, 8 + h:9 + h]),
                                     reads=[onb, b("gainH")], writes=[b("hT%d" % (8 + h))])
                            g = fin["g"]; fin["g"] += 1
                            flush_upto(g - 2)
                            steps = finish_steps(ob, None, ssq_fn, gain_fn)
                            deferred.append((i + LOOK + 5, steps[0], g, ob))
                            deferred.append((i + LOOK + 9, steps[1], g, -1))
                    else:
                        _, u, kv, ik, nk, slot, kind = blk
                        ob = 5
                        if ik == 0:
                            flush_bank(ob)
                        comp("pe", lambda E: E.matmul(
                            ps[ob][0:65, :], Va[:, slot, kv * 65:(kv + 1) * 65], PT[pk][:],
                            start=(ik == 0), stop=(ik == nk - 1)),
                            reads=[b("Va%d" % slot), b("Va_all"), b("PT%d" % pk)], writes=[bank[ob]])
                        if ik == nk - 1:
                            first = (u == 0 and kv == 0)

                            def ssq_fn(kq):
                                def fn(E):
                                    for g_ in range(4):
                                        ins = E.matmul(ps[6][:, u * 128:(u + 1) * 128], onesb[0:64, :],
                                                       sqbf[kq][0:64, g_ * 128:(g_ + 1) * 128],
                                                       start=(first and g_ == 0), stop=False, skip_group_check=True)
                                    return ins
                                comp("pe", fn, reads=[b("onesb"), b("sqbf%d" % kq)], writes=[bank[6]])

                            def gain_fn(on, onb):
                                for g_ in range(4):
                                    hs = kv * 4 + g_
                                    comp("dve", lambda E, g_=g_, hs=hs: E.tensor_scalar_mul(
                                        hT[0:64, hs, u * 128:(u + 1) * 128], on[:, g_ * 128:(g_ + 1) * 128], gainH[:, hs:hs + 1]),
                                        reads=[onb, b("gainH")], writes=[b("hT%d" % hs)])
                            g = fin["g"]; fin["g"] += 1
                            flush_upto(g - 2)
                            steps = finish_steps(ob, kv, ssq_fn, gain_fn)
                            deferred.append((i + LOOK + 5, steps[0], g, ob))
                            deferred.append((i + LOOK + 9, steps[1], g, -1))

                nb_ = len(blocks)
                for i in range(nb_ + LOOK):
                    if i < nb_:
                        emit_qk(i)
                    while deferred and deferred[0][0] <= i:
                        deferred.pop(0)[1]()
                    if i - LOOK >= 0:
                        emit_pv(i - LOOK)
                while deferred:
                    deferred.pop(0)[1]()
                hctr["fox"] = False
                comp("pool", lambda E: E.tensor_copy(KTa[:, 0, :], KTa[:, 4, :]), reads=[b("KTa4")], writes=[b("KTa0")])
                comp("pool", lambda E: E.tensor_copy(Va[:, 0, :], Va[:, 4, :]), reads=[b("Va4"), b("Va_all")], writes=[b("Va0")])
                for (rr_, bk_, nm_) in ((rrA, 6, "rrA"), (rrB, 7, "rrB")):
                    comp("dve", lambda E, rr_=rr_, bk_=bk_: E.tensor_scalar(out=rr_[:], in0=ps[bk_][:, :], scalar1=1.0 / 512,
                                                                           scalar2=RMS_EPS, op0=ALU.mult, op1=ALU.add),
                         reads=[bank[bk_]], writes=[b(nm_)])
                    comp("act", lambda E, rr_=rr_: E.activation(out=rr_[:], in_=rr_[:], func=AF.Ln), reads=[b(nm_)], writes=[b(nm_)])
                    comp("act", lambda E, rr_=rr_: E.activation(out=rr_[:], in_=rr_[:], func=AF.Exp, scale=-0.5),
                         reads=[b(nm_)], writes=[b(nm_)])

                if DB:
                    dump("att", lambda: hT[:, :, :], [b("hT%d" % j) for j in range(28)], BF16)
                    dump("KTa", lambda: KTa[:], [b("KTa%d" % a) for a in range(5)], BF16)
                    dump("Va", lambda: Va[:], [b("Va%d" % a) for a in range(5)] + [b("Va_all")], BF16)
                    dump("fsp", lambda: fsp[:], [b("fsp")])
                    dump("ncT", lambda: ncT[:], [b("ncT")])
                    dump("AUGT", lambda: AUGT[:], [b("AUGT")], BF16)
                    dump("Bcols", lambda: Bcols[:], [b("Bcols")])
                    dump("ncS", lambda: ncS[:], [b("ncS")])
                    dump("rrA", lambda: rrA[:], [b("rrA")])
                    dump("rrB", lambda: rrB[:], [b("rrB")])
                for hs in range(1, 16, 2):
                    pbk = (hs // 2) % 2
                    comp("pe", lambda E, hs=hs, pbk=pbk: E.matmul(ps[pbk][:, :], shiftI[:], hT[0:64, hs, :], start=True, stop=True),
                         reads=[b("shiftI"), b("hT%d" % hs)], writes=[bank[pbk]])
                    if pbk == 0:
                        comp("act", lambda E, hs=hs, pbk=pbk: E.copy(hT[64:128, hs - 1, :], ps[pbk][64:128, :]),
                             reads=[bank[pbk]], writes=[b("hT%d" % (hs - 1))])
                    else:
                        comp("dve", lambda E, hs=hs, pbk=pbk: E.tensor_copy(hT[64:128, hs - 1, :], ps[pbk][64:128, :]),
                             reads=[bank[pbk]], writes=[b("hT%d" % (hs - 1))])
                pend = None
                oTb = [b("hT%d" % hs) for hs in range(0, 16, 2)]
                wout_v2 = wout_d.rearrange("(c p) n -> p c n", p=128)
                for n in range(8):
                    sl = load_d("wo%d" % n, lambda w, n=n: [
                        (w[:, 0:1024].rearrange("p (c n) -> p c n", c=8), wout_v2[:, :, n * 128:(n + 1) * 128])], np_=128, nf=1024)
                    ya, yb2 = (0, 1) if n % 2 == 0 else (2, 3)

                    def fn(E, sl=sl, ya=ya, yb2=yb2):
                        wv = wd[sl][:, 0:1024].rearrange("p (c n) -> p c n", c=8)
                        for pr in range(4):
                            E.matmul(ps[ya][:, :], wv[:, pr, :], hT[:, 2 * pr, :], start=(pr == 0), stop=(pr == 3))
                        for pr in range(4, 8):
                            ins = E.matmul(ps[yb2][:, :], wv[:, pr, :], hT[:, 2 * pr, :], start=(pr == 4), stop=(pr == 7))
                        return ins
                    comp("pe", fn, reads=[b("wd%d" % sl)] + oTb, writes=[bank[ya], bank[yb2]])
                    if pend is not None:
                        comp(pend[0], pend[1], reads=pend[2], writes=pend[3])
                    pend = residual_chunk(bb, n, (ya, yb2), None, 7, first=(n == 0), rr=True)
                comp(pend[0], pend[1], reads=pend[2], writes=pend[3])
                layernorm(bb, 8, 9, 10, 11, final=False)

                if DB:
                    dump("xa2", lambda: xa[:], xab)
                ffn(2, bb, w2g_d, w2u_d, w2d_d, 12, 13, 14, None, None, final=True)

                nxt = (bb, it + 1) if it + 1 < NT else ((bb + 1, 0) if bb + 1 < NSEQ else None)
                if nxt is not None:
                    nb2, ni2 = nxt
                    for u in range(2):
                        dma("sp", s_x[u], [(xin[u][:], x_d[nb2, ni2 * TT + u * 128:ni2 * TT + (u + 1) * 128, :])],
                            writes=[b("xin%d" % u)])
                        preloaded.add((nb2, ni2, u))
                for u in range(4):
                    for half in range(2):
                        pb = half

                        def fn(E, u=u, half=half, pb=pb):
                            for cc in range(4):
                                c = half * 4 + cc
                                ins = E.transpose(ps[pb][:, cc * 128:(cc + 1) * 128], xa[:, c, u * 128:(u + 1) * 128], ident[:])
                            return ins
                        comp("pe", fn, reads=[b("xa%d" % c) for c in range(half * 4, half * 4 + 4)] + [b("ident")], writes=[bank[pb]])
                        if half == 0:
                            comp("act", lambda E, pb=pb: E.copy(tmpf[0][:], ps[pb][:, :]),
                                 reads=[bank[pb]], writes=[b("tmpf0")])
                        else:
                            comp("dve", lambda E, pb=pb: E.tensor_copy(tmpf[1][:], ps[pb][:, :]),
                                 reads=[bank[pb]], writes=[b("tmpf1")])
                        dma("sp", s_o[half], [(out_d[bb, t0 + u * 128:t0 + (u + 1) * 128, half * 512:(half + 1) * 512], tmpf[half][:])],
                            reads=[b("tmpf%d" % half)])

        P.emit(es, final_waits=list(s_o) + dbg_sems)
    return nc


def _consts():
    ident = np.eye(128, dtype=np.float32)
    s = np.arange(128)[:, None]
    q = np.arange(128)[None, :]
    maskneg = np.where(s > q, NEG, 0.0).astype(np.float32)
    slopes = 2.0 ** (-8.0 * np.arange(1, 9) / 8.0)
    biasA = np.zeros((4, 128, 512), np.float32)
    for kv in range(2):
        for kind in range(2):
            for g in range(4):
                sl = slopes[kv * 4 + g]
                if kind == 1:
                    dist = q - s
                    valid = dist >= 0
                else:
                    dist = q - s + 128
                    valid = dist < 128
                biasA[kv * 2 + kind, :, g * 128:(g + 1) * 128] = np.where(valid, -sl * dist, NEG)
    sel8 = np.zeros((128, 8 * 128), np.float32)
    for h in range(8):
        sel8[h, h * 128:(h + 1) * 128] = 1.0
    sel64 = np.zeros((128, 64), np.float32)
    sel64[64, :] = 1.0
    shiftI = np.zeros((64, 128), np.float32)
    shiftI[np.arange(64), 64 + np.arange(64)] = 1.0
    return dict(ident=ident, maskneg=maskneg, biasA=biasA, sel8=sel8, sel64=sel64, shiftI=shiftI)


_NC_CACHE = {}


def _run(inputs, ncores, NSEQ, SEQ):
    key = (NSEQ, SEQ)
    if key not in _NC_CACHE:
        _NC_CACHE[key] = build_nc(NSEQ, SEQ)
    nc = _NC_CACHE[key]
    f = lambda a: np.ascontiguousarray(np.asarray(a, dtype=np.float32))
    x = f(inputs["x"]); c = f(inputs["c"])
    pvec = np.concatenate([f(inputs["b_ada"]).reshape(72, 128), f(inputs["ln1_g"]).reshape(8, 128),
                           f(inputs["ln1_b"]).reshape(8, 128), f(inputs["ln2_g"]).reshape(8, 128),
                           f(inputs["ln2_b"]).reshape(8, 128), f(inputs["ln3_g"]).reshape(8, 128),
                           f(inputs["ln3_b"]).reshape(8, 128), f(inputs["grp_gain"]).reshape(8, 128)], axis=0)
    shared = dict(
        w_ada=f(inputs["w_ada"])[0], pvec=np.ascontiguousarray(pvec),
        gain16=f(inputs["grp_gain"]).reshape(16, 64), b_forget=f(inputs["b_forget"]).reshape(8, 1),
        sinks=f(inputs["swa_sinks"]).reshape(1, 8),
        w1g=f(inputs["ffn1_w_gate"])[0], w1u=f(inputs["ffn1_w_up"])[0], w1d=f(inputs["ffn1_w_down"])[0],
        w2g=f(inputs["ffn2_w_gate"])[0], w2u=f(inputs["ffn2_w_up"])[0], w2d=f(inputs["ffn2_w_down"])[0],
        w_in=f(inputs["w_in"])[0], w_out=f(inputs["w_out"])[0],
        w_f128=np.ascontiguousarray(np.concatenate([f(inputs["w_in"])[0][:, 2304:2312], f(inputs["w_in"])[0][:, 2184:2304]], axis=1)))
    shared.update(_consts())
    in_maps = []
    for i in range(ncores):
        m = dict(shared)
        m["x"] = np.ascontiguousarray(x[i * NSEQ:(i + 1) * NSEQ])
        m["c16"] = np.ascontiguousarray(c[i * NSEQ:(i + 1) * NSEQ].reshape(NSEQ * 8, 128))
        in_maps.append(m)
    res = run_bass_kernel_spmd(nc, in_maps, core_ids=list(range(ncores)))
    global _LAST
    _LAST = res.results
    return np.concatenate([np.asarray(r["out"]) for r in res.results], axis=0).astype(np.float32)


def kernel(**inputs):
    return _run(inputs, 8, 2, 4096)
```
